# Optimizing a Trainium2 kernel written in Bass

```python
import jax, jax.numpy as jnp
from jax import lax
import numpy as np

D_MODEL = 1024
BATCH = 8
SEQ = 4096
DEPTH = 2
DEC_BATCH = 16
DEC_SEQ = 4096
PAST_LEN = 128

N_MIXERS = 2
N_FGROUPS = 8
F_GROUP = D_MODEL // N_FGROUPS
HEAD_DIM = 128
N_HEADS = D_MODEL // HEAD_DIM
N_KV_HEADS = 2
KV_GROUP = N_HEADS // N_KV_HEADS
QKV_DIM = (N_HEADS + 2 * N_KV_HEADS) * HEAD_DIM
AXIS_DIM = HEAD_DIM // 2
ROPE_THETA = 10000.0
GRID_W = 64
Q_BLOCK = 128
D_FF = 4 * D_MODEL
EPS = 1e-6

kernel_name = "fourier_gqa_axial_hybrid_encoder"


def rms_norm(x, g):
    x32 = x.astype(jnp.float32)
    y = x32 * lax.rsqrt(jnp.mean(x32 * x32, axis=-1, keepdims=True) + EPS)
    return (y * g.astype(jnp.float32)).astype(x.dtype)


def fourier_mix(h, w_out):
    B, S, _ = h.shape
    hg = h.astype(jnp.float32).reshape(B, S, N_FGROUPS, F_GROUP)
    f = jnp.fft.fft2(hg, axes=(1, 3), norm="ortho").real
    f = f.reshape(B, S, D_MODEL).astype(h.dtype)
    return f @ w_out


def axial_angles(S):
    rows = S // GRID_W
    row = jnp.repeat(jnp.arange(rows, dtype=jnp.float32), GRID_W)
    col = jnp.tile(jnp.arange(GRID_W, dtype=jnp.float32), rows)
    inv = ROPE_THETA ** (-jnp.arange(0, AXIS_DIM, 2, dtype=jnp.float32) / AXIS_DIM)
    return row[:, None] * inv[None, :], col[:, None] * inv[None, :]


def rope_1d(x, ang):
    c = jnp.cos(ang)[:, None, :]
    s = jnp.sin(ang)[:, None, :]
    x1, x2 = jnp.split(x, 2, axis=-1)
    return jnp.concatenate([x1 * c - x2 * s, x2 * c + x1 * s], axis=-1)


def axial_rope(x, ang_row, ang_col):
    x32 = x.astype(jnp.float32)
    xr, xc = jnp.split(x32, 2, axis=-1)
    return jnp.concatenate([rope_1d(xr, ang_row), rope_1d(xc, ang_col)], axis=-1)


def gqa_attention(h, w_qkv, q_gain, k_gain, w_o):
    B, S, _ = h.shape
    qkv = h @ w_qkv
    q = qkv[..., : N_HEADS * HEAD_DIM].reshape(B, S, N_HEADS, HEAD_DIM)
    k = qkv[..., N_HEADS * HEAD_DIM:(N_HEADS + N_KV_HEADS) * HEAD_DIM].reshape(B, S, N_KV_HEADS, HEAD_DIM)
    v = qkv[..., (N_HEADS + N_KV_HEADS) * HEAD_DIM:].reshape(B, S, N_KV_HEADS, HEAD_DIM)
    q = rms_norm(q, q_gain)
    k = rms_norm(k, k_gain)
    ang_row, ang_col = axial_angles(S)
    q = axial_rope(q, ang_row, ang_col) * (HEAD_DIM ** -0.5)
    k = axial_rope(k, ang_row, ang_col)
    n_blk = S // Q_BLOCK
    qb = q.reshape(B, n_blk, Q_BLOCK, N_KV_HEADS, KV_GROUP, HEAD_DIM).transpose(1, 0, 2, 3, 4, 5)

    def one_block(q_blk):
        s = jnp.einsum("bqkgd,bskd->bkgqs", q_blk, k)
        p = jax.nn.softmax(s, axis=-1)
        return jnp.einsum("bkgqs,bskd->bqkgd", p.astype(v.dtype), v)

    o = lax.map(one_block, qb)
    o = o.transpose(1, 0, 2, 3, 4, 5).reshape(B, S, N_HEADS * HEAD_DIM).astype(h.dtype)
    return o @ w_o


def sq_relu_mlp(h, w_up, w_down):
    a = jnp.maximum(h @ w_up, 0)
    return (a * a) @ w_down


def trunk(x, fourier_norm, fourier_w_out, attn_norm, attn_w_qkv, attn_q_norm, attn_k_norm,
          attn_w_o, mlp_norm, mlp_w_up, mlp_w_down, final_norm):
    for i in range(DEPTH):
        j = i // N_MIXERS
        if i % N_MIXERS == 0:
            x = x + fourier_mix(rms_norm(x, fourier_norm[j]), fourier_w_out[j])
        else:
            x = x + gqa_attention(rms_norm(x, attn_norm[j]), attn_w_qkv[j], attn_q_norm[j],
                                  attn_k_norm[j], attn_w_o[j])
        x = x + sq_relu_mlp(rms_norm(x, mlp_norm[i]), mlp_w_up[i], mlp_w_down[i])
    return rms_norm(x, final_norm)


def setup_inputs(seed: int = 0) -> dict:
    key = jax.random.key(seed)
    ks = jax.random.split(key, 16)
    n_a = (DEPTH + N_MIXERS - 1) // N_MIXERS
    n_b = DEPTH // N_MIXERS
    f32 = jnp.float32

    def w(k, shape, fan_in):
        return jax.random.normal(k, shape, f32) * (fan_in ** -0.5)

    def gain(k, shape):
        return 1.0 + 0.02 * jax.random.normal(k, shape, f32)

    return {
        "x_prompt": jax.random.normal(ks[0], (BATCH, SEQ, D_MODEL), f32),
        "x_sample": jax.random.normal(ks[1], (DEC_BATCH, DEC_SEQ, D_MODEL), f32),
        "fourier_norm": gain(ks[2], (n_a, D_MODEL)),
        "fourier_w_out": w(ks[3], (n_a, D_MODEL, D_MODEL), D_MODEL),
        "attn_norm": gain(ks[4], (n_b, D_MODEL)),
        "attn_w_qkv": w(ks[5], (n_b, D_MODEL, QKV_DIM), D_MODEL),
        "attn_q_norm": gain(ks[6], (n_b, HEAD_DIM)),
        "attn_k_norm": gain(ks[7], (n_b, HEAD_DIM)),
        "attn_w_o": w(ks[8], (n_b, N_HEADS * HEAD_DIM, D_MODEL), N_HEADS * HEAD_DIM),
        "mlp_norm": gain(ks[9], (DEPTH, D_MODEL)),
        "mlp_w_up": w(ks[10], (DEPTH, D_MODEL, D_FF), D_MODEL),
        "mlp_w_down": w(ks[11], (DEPTH, D_FF, D_MODEL), D_FF),
        "final_norm": gain(ks[12], (D_MODEL,)),
    }


def reference(x_prompt, x_sample, fourier_norm, fourier_w_out, attn_norm, attn_w_qkv, attn_q_norm,
              attn_k_norm, attn_w_o, mlp_norm, mlp_w_up, mlp_w_down, final_norm):
    y_prompt = trunk(x_prompt, fourier_norm, fourier_w_out, attn_norm, attn_w_qkv, attn_q_norm,
                     attn_k_norm, attn_w_o, mlp_norm, mlp_w_up, mlp_w_down, final_norm)
    y_sample = trunk(x_sample, fourier_norm, fourier_w_out, attn_norm, attn_w_qkv, attn_q_norm,
                     attn_k_norm, attn_w_o, mlp_norm, mlp_w_up, mlp_w_down, final_norm)
    return (y_prompt, y_sample)
```

```python
import numpy as np
from contextlib import ExitStack
import concourse.bass as bass
import concourse.mybir as mybir
from concourse.bass_utils import run_bass_kernel_spmd

F32 = mybir.dt.float32
BF16 = mybir.dt.bfloat16
AF = mybir.ActivationFunctionType
ALU = mybir.AluOpType
AX = mybir.AxisListType

D = 1024
S = 4096
NCORES = 8
SEQ_PER_CORE = 3
EPS = 1e-6
ENGS = ("sp", "act", "pool", "dve", "pe")
ARENA_WORDS = 52000


class _Stop(Exception):
    pass


def chk(n):
    return None


class Tok:
    __slots__ = ("sem", "val", "eng")

    def __init__(self, sem, val, eng):
        self.sem = sem
        self.val = val
        self.eng = eng


class Prog:
    def __init__(self, nc, es):
        self.nc = nc
        self.es = es
        self.q = {e: [] for e in ENGS}
        self.tl = {e: es.enter_context(nc.semaphore("tl_" + e)) for e in ("act", "pool", "dve", "pe")}
        self.tlc = {e: 0 for e in self.tl}
        self.dsem = {}
        self.dcnt = {}
        self.waited = {e: {} for e in ENGS}
        self.last_dma = {}

    def _waits(self, eng, waits):
        best = {}
        for t in waits:
            if t is None:
                continue
            if eng == "pe" and t.eng == "pe":
                continue
            k = id(t.sem)
            if k not in best or best[k].val < t.val:
                best[k] = t
        out = []
        for k, t in best.items():
            if self.waited[eng].get(k, 0) >= t.val:
                continue
            self.waited[eng][k] = t.val
            out.append((t.sem, t.val))
        return out

    def op(self, eng, fn, waits=(), signal=True):
        w = self._waits(eng, waits)
        tok = None
        inc = None
        if signal:
            self.tlc[eng] += 1
            tok = Tok(self.tl[eng], self.tlc[eng], eng)
            inc = (self.tl[eng], 1)
        self.q[eng].append((w, fn, inc))
        return tok

    def dma(self, eng, key, fn, waits=()):
        w = self._waits(eng, waits)
        if key not in self.dsem:
            self.dsem[key] = self.es.enter_context(self.nc.semaphore("d_" + key))
            self.dcnt[key] = 0
        sem = self.dsem[key]
        self.dcnt[key] += 16
        tok = Tok(sem, self.dcnt[key], "dma")
        self.q[eng].append((w, fn, (sem, 16)))
        self.last_dma[key] = tok
        return tok

    def wait_only(self, eng, waits):
        w = self._waits(eng, waits)
        if w:
            self.q[eng].append((w, None, None))

    def last(self, eng):
        if self.tlc[eng] == 0:
            return None
        return Tok(self.tl[eng], self.tlc[eng], eng)

    def barrier(self):
        toks = [self.last(e) for e in ("act", "pool", "dve", "pe")]
        toks += list(self.last_dma.values())
        for e in ENGS:
            self.wait_only(e, toks)

    def replay(self, block):
        def mk(engname):
            def body(e):
                for (w, fn, inc) in self.q[engname]:
                    for (sem, val) in w:
                        e.wait_ge(sem, val)
                    if fn is None:
                        continue
                    ins = fn(e)
                    if inc is not None:
                        ins.then_inc(inc[0], inc[1])
            return body
        block.sync(mk("sp"))
        block.scalar(mk("act"))
        block.gpsimd(mk("pool"))
        block.vector(mk("dve"))
        block.tensor(mk("pe"))


class Arena:
    def __init__(self, big, nwords):
        self.big = big
        self.n = nwords
        self.off = 0

    def mark(self):
        return self.off

    def reset(self, m):
        self.off = m

    def alloc(self, dt, shape):
        nel = int(np.prod(shape))
        nw = (nel * (4 if dt == F32 else 2) + 3) // 4
        nw = (nw + 7) // 8 * 8
        assert self.off + nw <= self.n, f"arena overflow {self.off}+{nw}>{self.n}"
        a = self.big[:, self.off:self.off + nw]
        self.off += nw
        if dt != F32:
            a = a.bitcast(dt)
        a = a[:, 0:nel]
        if len(shape) == 2:
            a = a.rearrange("p (a b) -> p a b", b=shape[1])
        elif len(shape) == 3:
            a = a.rearrange("p (a b c) -> p a b c", b=shape[1], c=shape[2])
        elif len(shape) == 4:
            a = a.rearrange("p (a b c d) -> p a b c d", b=shape[1], c=shape[2], d=shape[3])
        return a


_CONSTS = None


def host_consts():
    global _CONSTS
    if _CONSTS is not None:
        return _CONSTS
    Q = 1024
    s = np.arange(Q, dtype=np.int64)
    ang = 2.0 * np.pi * ((np.outer(s, s) % Q).astype(np.float64)) / Q
    cq = np.cos(ang).astype(np.float32)
    sq = np.sin(ang).astype(np.float32)
    c = np.arange(128, dtype=np.int64)
    angc = 2.0 * np.pi * ((np.outer(c, c) % 128).astype(np.float64)) / 128
    norm = 1.0 / np.sqrt(4096.0 * 128.0)
    Cc = np.cos(angc) * norm
    Sc = np.sin(angc) * norm
    rmat = np.concatenate([Cc, -Sc, -Sc, -Cc, Sc, Cc], axis=1).astype(np.float32)
    p = np.arange(128)[:, None, None]
    r = np.arange(4)[None, :, None]
    j = np.arange(8)[None, None, :]
    angt = 2.0 * np.pi * ((j * 128 + p) * r % 4096).astype(np.float64) / 4096.0
    tw = np.concatenate([np.cos(angt).reshape(128, 32), np.sin(angt).reshape(128, 32)], axis=1).astype(np.float32)
    inv = 10000.0 ** (-np.arange(0, 64, 2, dtype=np.float64) / 64.0)
    tok = (np.arange(32)[None, :] * 128 + np.arange(128)[:, None]).astype(np.float64)
    row = np.floor(tok / 64.0)
    col = tok - row * 64.0
    ang_r = (row.astype(np.float32)[:, :, None] * inv.astype(np.float32)[None, None, :]).astype(np.float32)
    ang_c = (col.astype(np.float32)[:, :, None] * inv.astype(np.float32)[None, None, :]).astype(np.float32)
    angs = np.stack([ang_r, ang_c], axis=2).astype(np.float64)
    ropec = np.cos(angs).astype(np.float32).reshape(128, 32 * 64)
    ropes = np.sin(angs).astype(np.float32).reshape(128, 32 * 64)
    ident = np.eye(128, dtype=np.float32)
    _CONSTS = dict(cq=cq, sq=sq, rmat=rmat, tw=tw, ropec=ropec, ropes=ropes, ropen=(-ropes).copy(), ident=ident)
    return _CONSTS


def build_program(n_seq=SEQ_PER_CORE, phases=("F", "M0", "A", "M1"), x_from_y=False, T_override=None):
    nc = bass.Bass("TRN2", target_bir_lowering=False)
    T = n_seq * S if T_override is None else T_override

    def din(name, shape, dt=F32):
        return nc.dram_tensor(name, list(shape), dt, kind="ExternalInput").ap()

    x_d = din("x", [T, D])
    y_d = nc.dram_tensor("y", [T, D], F32, kind="ExternalOutput").ap()
    gF_d = din("g_f", [128, 8])
    gA_d = din("g_a", [128, 8])
    gM_d = din("g_m", [128, 16])
    gfin_d = din("g_fin", [1, D])
    qg_d = din("qg", [1, 128])
    kg_d = din("kg", [1, 128])
    wout_d = din("w_fout", [D, D])
    wqkv_d = din("w_qkv", [D, 1536])
    wo_d = din("w_o", [D, D])
    wup_d = din("w_up", [2, D, 4096])
    wdn_d = din("w_dn", [2, 4096, D])
    cq_d = din("cq", [1024, 1024])
    sq_d = din("sq", [1024, 1024])
    rmat_d = din("rmat", [128, 768])
    tw_d = din("tw", [128, 64])
    ropec_d = din("ropec", [128, 2048])
    ropes_d = din("ropes", [128, 2048])
    ropen_d = din("ropen", [128, 2048])
    ident_d = din("ident", [128, 128])
    qscr = nc.dram_tensor("qscr", [8, 128, 8 * 512], BF16).ap()

    with ExitStack() as es:
        big = es.enter_context(nc.sbuf_tensor("arena", [128, ARENA_WORDS], F32))
        ps = [es.enter_context(nc.psum_tensor(f"ps{i}", [128, 512], F32)) for i in range(8)]
        P = Prog(nc, es)
        A = Arena(big, ARENA_WORDS)

        ident = A.alloc(BF16, [128])
        ones = A.alloc(BF16, [128])
        epsc = A.alloc(F32, [8])
        gF = A.alloc(F32, [8])
        gA = A.alloc(F32, [8])
        gM = A.alloc(F32, [16])
        t_c = [
            P.dma("pool", "cst", lambda e: e.dma_start(out=ident, in_=ident_d[:, :])),
            P.dma("pool", "cst", lambda e: e.dma_start(out=gF, in_=gF_d[:, :])),
            P.dma("pool", "cst", lambda e: e.dma_start(out=gA, in_=gA_d[:, :])),
            P.dma("pool", "cst", lambda e: e.dma_start(out=gM, in_=gM_d[:, :])),
        ]
        t_c.append(P.op("pool", lambda e: e.memset(ones, 1.0)))
        t_c.append(P.op("pool", lambda e: e.memset(epsc, EPS)))
        P.barrier()
        base_mark = A.mark()

        def psb(i):
            return ps[i][:].bitcast(BF16)

        def rows(ap_d, r0, nt):
            return ap_d[r0:r0 + nt * 128, :].rearrange("(t p) d -> p t d", p=128)

        def norm_block(xb, hn, st, waits, hn_free):
            t_sq = None
            for t in range(2):
                t_sq = P.op("act", lambda e, t=t: e.activation(out=hn[:, t, :], in_=xb[:, t, :], func=AF.Square,
                                                               accum_out=st[:, t:t + 1]), list(waits) + list(hn_free))
            t_rt = P.op("act", lambda e: e.activation(out=st[:, 2:4], in_=st[:, 0:2], func=AF.Sqrt,
                                                      scale=1.0 / D, bias=epsc[:, 0:1]), [t_sq])
            t_rc = P.op("dve", lambda e: e.reciprocal(out=st[:, 4:6], in_=st[:, 2:4]), [t_rt])
            toks = []
            for t in range(2):
                toks.append(P.op("dve", lambda e, t=t: e.tensor_scalar(out=hn[:, t, :], in0=xb[:, t, :],
                                                                       scalar1=st[:, 4 + t:5 + t], scalar2=None,
                                                                       op0=ALU.mult), [t_rc, t_sq]))
            return toks[-1]

        def transpose_block(hn, hT, gain, banks, waits, hT_free, bank_free):
            pe_tok = None
            for c in range(8):
                pv = psb(banks[c // 4])
                for t in range(2):
                    last = (c % 4 == 3 and t == 1)
                    tk = P.op("pe", lambda e, c=c, t=t, pv=pv: e.transpose(
                        out=pv[:, (c % 4) * 256 + t * 128:(c % 4) * 256 + t * 128 + 128],
                        in_=hn[:, t, c * 128:(c + 1) * 128], identity=ident),
                        list(waits) + list(bank_free), signal=last)
                    if last:
                        pe_tok = tk
                        if c == 3:
                            pe_tok_a = tk
            ev = []
            for c in range(8):
                pv = psb(banks[c // 4])
                src = pv[:, (c % 4) * 256:(c % 4) * 256 + 256]
                w = [pe_tok_a if c < 4 else pe_tok] + list(hT_free)
                if c < 4:
                    ev.append(P.op("act", lambda e, c=c, src=src: e.activation(out=hT[:, c, :], in_=src, func=AF.Copy,
                                                                               scale=gain[:, c:c + 1]), w))
                else:
                    ev.append(P.op("dve", lambda e, c=c, src=src: e.tensor_scalar(out=hT[:, c, :], in0=src,
                                                                                  scalar1=gain[:, c:c + 1], scalar2=None,
                                                                                  op0=ALU.mult), w))
            return pe_tok, ev

        def load_weight(dst, src_ap, nsplit, key):
            nch = dst.shape[1]
            per = nch // nsplit
            toks = []
            for i in range(nsplit):
                toks.append(P.dma("pool", key, lambda e, i=i: e.dma_start(
                    out=dst[:, i * per:(i + 1) * per, :],
                    in_=src_ap[i * per * 128:(i + 1) * per * 128, :].rearrange("(c p) f -> p c f", p=128))))
            return toks

        def phase_mlp(L, final):
            A.reset(base_mark)
            wup = A.alloc(BF16, [8, 4096])
            wdn = A.alloc(BF16, [32, 1024])
            xb = [A.alloc(F32, [2, 1024]) for _ in range(3)]
            hn = A.alloc(BF16, [2, 1024])
            hT = A.alloc(BF16, [8, 256])
            aT = A.alloc(BF16, [32, 256])
            rl = [A.alloc(F32, [512]) for _ in range(3)]
            st = [A.alloc(F32, [8]) for _ in range(2)]
            st2 = A.alloc(F32, [8])
            junk = A.alloc(BF16, [1024])
            gfin = A.alloc(F32, [1024]) if final else None
            gain = gM[:, L * 8:(L + 1) * 8]
            wu_tok = []
            for j in range(4):
                wu_tok.append(P.dma("pool", "wu%d" % j, lambda e, j=j: e.dma_start(
                    out=wup[:, :, j * 1024:(j + 1) * 1024],
                    in_=wup_d[L].rearrange("(c p) f -> p c f", p=128)[:, :, j * 1024:(j + 1) * 1024])))
            wd_tok = []
            for hf in range(2):
                wd_tok.append(P.dma("pool", "wd%d" % hf, lambda e, hf=hf: e.dma_start(
                    out=wdn[:, :, hf * 512:(hf + 1) * 512],
                    in_=wdn_d[L].rearrange("(c p) f -> p c f", p=128)[:, :, hf * 512:(hf + 1) * 512])))
            wt = []
            if final:
                wt.append(P.dma("pool", "wg", lambda e: e.dma_start(out=gfin, in_=gfin_d[0:1, :].partition_broadcast(128))))
            nblk = T // 256
            TB = (0, 1)
            UB = (2, 3, 4)
            DB = (5, 6, 7)
            ld = [None] * nblk
            stt = [None] * nblk
            nrm = [None] * nblk
            hTr = [None] * nblk
            state = dict(hn_free=[], hT_free=[], tb_free=[], up_last=None, dn_last=None,
                         up_slot_free=[None] * 3, rl_free=[None] * 3, db_free=[None] * 3, dcount=0, ucount=0)

            def load(i):
                w = [stt[i - 3]] if i >= 3 else []
                ld[i] = P.dma("sp", "xl%d" % (i % 3), lambda e, i=i: e.dma_start(out=xb[i % 3], in_=rows(y_d, i * 256, 2)), w)

            def norm(i):
                nrm[i] = norm_block(xb[i % 3], hn, st[i % 2], [ld[i]], state["hn_free"])

            def transp(i):
                pe_tok, ev = transpose_block(hn, hT, gain, TB, [nrm[i]], state["hT_free"], state["tb_free"])
                state["hn_free"] = [pe_tok]
                state["tb_free"] = ev
                hTr[i] = ev

            def up(i):
                aT_free = [state["dn_last"]]
                sq_toks = []
                for fp in range(16):
                    slot = state["ucount"] % 3
                    state["ucount"] += 1
                    bank = ps[UB[slot]]
                    mm = None
                    for f2 in range(2):
                        fc = fp * 2 + f2
                        pv = bank[:, f2 * 256:(f2 + 1) * 256]
                        for c in range(8):
                            mm = P.op("pe", lambda e, c=c, fc=fc, pv=pv: e.matmul(pv, lhsT=wup[:, c, fc * 128:(fc + 1) * 128],
                                                                                  rhs=hT[:, c, :], start=(c == 0), stop=(c == 7)),
                                      hTr[i] + [wu_tok[fc // 8], state["up_slot_free"][slot]], signal=(c == 7 and f2 == 1))
                    r = P.op("act", lambda e, bank=bank, slot=slot: e.activation(out=rl[slot], in_=bank[:], func=AF.Relu),
                             [mm, state["rl_free"][slot]])
                    state["up_slot_free"][slot] = r
                    q = P.op("dve", lambda e, fp=fp, slot=slot: e.tensor_tensor(
                        out=aT[:, fp * 2:fp * 2 + 2, :].rearrange("p a b -> p (a b)"), in0=rl[slot], in1=rl[slot],
                        op=ALU.mult), [r] + aT_free)
                    state["rl_free"][slot] = q
                    sq_toks.append(q)
                    sq_toks.append(q)
                state["hT_free"] = [mm]
                return sq_toks

            def down_group(i, g, sq_toks):
                t, half = g // 2, g % 2
                k = state["dcount"] % 3
                state["dcount"] += 1
                pv = ps[DB[k]][:]
                mm = None
                for fc in range(32):
                    mm = P.op("pe", lambda e, fc=fc, t=t, half=half, pv=pv: e.matmul(
                        pv, lhsT=aT[:, fc, t * 128:(t + 1) * 128], rhs=wdn[:, fc, half * 512:(half + 1) * 512],
                        start=(fc == 0), stop=(fc == 31)), [sq_toks[fc], state["db_free"][k], wd_tok[half]], signal=(fc == 31))
                state["dn_last"] = mm
                xs = xb[i % 3][:, t, half * 512:(half + 1) * 512]
                ev = P.op("dve", lambda e, xs=xs, pv=pv: e.tensor_tensor(out=xs, in0=pv, in1=xs, op=ALU.add), [mm])
                state["db_free"][k] = ev
                return ev

            def finish_sub(i, t, evs):
                if not final:
                    return evs
                xs = xb[i % 3][:, t, :]
                a = P.op("act", lambda e: e.activation(out=junk, in_=xs, func=AF.Square, accum_out=st2[:, t:t + 1]), evs)
                b = P.op("act", lambda e: e.activation(out=st2[:, 2 + t:3 + t], in_=st2[:, t:t + 1], func=AF.Sqrt,
                                                       scale=1.0 / D, bias=epsc[:, 0:1]), [a])
                c = P.op("dve", lambda e: e.reciprocal(out=st2[:, 4 + t:5 + t], in_=st2[:, 2 + t:3 + t]), [b])
                d = P.op("dve", lambda e: e.scalar_tensor_tensor(out=xs, in0=xs, scalar=st2[:, 4 + t:5 + t], in1=gfin,
                                                                 op0=ALU.mult, op1=ALU.mult), [c, a] + wt)
                return [d]

            load(0)
            if nblk > 1:
                load(1)
            norm(0)
            transp(0)
            if nblk > 1:
                norm(1)
            for i in range(nblk):
                if i + 2 < nblk:
                    load(i + 2)
                sq_toks = up(i)
                e0 = down_group(i, 0, sq_toks)
                if i + 1 < nblk:
                    transp(i + 1)
                e1 = down_group(i, 1, sq_toks)
                f0 = finish_sub(i, 0, [e0, e1])
                e2 = down_group(i, 2, sq_toks)
                e3 = down_group(i, 3, sq_toks)
                f1 = finish_sub(i, 1, [e2, e3])
                stt[i] = P.dma("sp", "xs%d" % (i % 3), lambda e, i=i: e.dma_start(out=rows(y_d, i * 256, 2), in_=xb[i % 3]),
                               f0 + f1)
                if i + 2 < nblk:
                    norm(i + 2)
            P.barrier()

        fstate = {}

        def phase_fourier_setup():
            A.reset(base_mark)
            f = fstate
            f["cq"] = A.alloc(BF16, [8, 1024])
            f["sq"] = A.alloc(BF16, [8, 1024])
            f["wout"] = A.alloc(BF16, [8, 1024])
            f["rmat"] = A.alloc(BF16, [768])
            f["tw"] = A.alloc(F32, [64])
            f["hT"] = A.alloc(BF16, [8, 4096])
            f["xb"] = [A.alloc(F32, [2, 1024]) for _ in range(2)]
            f["hn"] = A.alloc(BF16, [2, 1024])
            f["hn2"] = A.alloc(BF16, [2, 1024])
            f["hTb"] = A.alloc(BF16, [8, 256])
            f["st"] = [A.alloc(F32, [8]) for _ in range(2)]
            f["cmb"] = [A.alloc(BF16, [4, 1024]) for _ in range(2)]
            f["tmpa"] = A.alloc(F32, [1024])
            f["tmpb"] = A.alloc(F32, [1024])
            f["zp"] = [A.alloc(BF16, [8, 2, 128]) for _ in range(4)]
            f["t12"] = [A.alloc(F32, [4, 128]) for _ in range(2)]
            wt = load_weight(f["cq"], cq_d, 4, "w")
            wt += load_weight(f["sq"], sq_d, 4, "w")
            wt += load_weight(f["wout"], wout_d, 4, "w")
            wt.append(P.dma("pool", "w", lambda e: e.dma_start(out=f["rmat"], in_=rmat_d[:, :])))
            wt.append(P.dma("pool", "w", lambda e: e.dma_start(out=f["tw"], in_=tw_d[:, :])))
            f["wt"] = wt

        def phase_fourier(sq_i, x_src):
            f = fstate
            wt = f["wt"]
            hT, hn, hTb, cq, sq, wout, rmat, tw = f["hT"], f["hn"], f["hTb"], f["cq"], f["sq"], f["wout"], f["rmat"], f["tw"]
            xb = f["xb"]
            r0 = sq_i * S
            nblk = S // 256
            TB = (0, 1)
            ld = [None] * nblk
            nrm = [None] * nblk
            cp = [None] * nblk
            st_ = dict(hn_free=[[], []], tb_free=[[], []], xfree=[None, None])
            hns = [hn, f["hn2"]]
            TBs = [(0, 1), (2, 3)]

            def load(i):
                ld[i] = P.dma("sp", "xl%d" % (i % 2), lambda e, i=i: e.dma_start(out=xb[i % 2], in_=rows(x_src, r0 + i * 256, 2)),
                              [st_["xfree"][i % 2]])

            def norm(i):
                nrm[i] = norm_block(xb[i % 2], hns[i % 2], f["st"][i % 2], [ld[i]], st_["hn_free"][i % 2])
                st_["xfree"][i % 2] = nrm[i]

            load(0)
            load(1)
            norm(0)
            for i in range(nblk):
                if i + 1 < nblk:
                    norm(i + 1)
                hn_i = hns[i % 2]
                TBi = TBs[i % 2]
                pe_tok = None
                pe_tok_a = None
                for c in range(8):
                    pv = psb(TBi[c // 4])
                    for t in range(2):
                        lastm = (c % 4 == 3 and t == 1)
                        tk = P.op("pe", lambda e, c=c, t=t, pv=pv, hn_i=hn_i: e.transpose(
                            out=pv[:, (c % 4) * 256 + t * 128:(c % 4) * 256 + t * 128 + 128],
                            in_=hn_i[:, t, c * 128:(c + 1) * 128], identity=ident),
                            [nrm[i]] + st_["tb_free"][i % 2] + t_c, signal=lastm)
                        if lastm:
                            pe_tok = tk
                            if c == 3:
                                pe_tok_a = tk
                st_["hn_free"][i % 2] = [pe_tok]
                ev = []
                for c in range(8):
                    pv = psb(TBi[c // 4])
                    src = pv[:, (c % 4) * 256:(c % 4) * 256 + 256]
                    w = [pe_tok_a if c < 4 else pe_tok]
                    dst = hT[:, c, i * 256:(i + 1) * 256]
                    if c < 4:
                        ev.append(P.op("act", lambda e, c=c, src=src, dst=dst: e.activation(out=dst, in_=src, func=AF.Copy,
                                                                                            scale=gF[:, c:c + 1]), w))
                    else:
                        ev.append(P.op("dve", lambda e, c=c, src=src, dst=dst: e.tensor_scalar(out=dst, in0=src,
                                                                                               scalar1=gF[:, c:c + 1], scalar2=None,
                                                                                               op0=ALU.mult), w))
                st_["tb_free"][i % 2] = ev
                if i + 2 < nblk:
                    load(i + 2)
            hT_done = [P.last("act"), P.last("dve")]

            ZB = (0, 1, 2, 3)
            FB = (4, 5, 6, 7)
            zp = f["zp"]
            F2 = dict(zp_free=[[] for _ in range(4)], zb_free=[None] * 4, fb_free=[None] * 4, cmb_free=[[], []],
                      fcount=0, zcount=0, tcount=0, t12_free=[None, None], cm_ready={}, last_seq={})

            def butterflies(g):
                cm = f["cmb"][g % 2]
                ta, tb_ = f["tmpa"], f["tmpb"]
                h = [hT[:, g, q * 1024:(q + 1) * 1024] for q in range(4)]
                w0 = hT_done + F2["cmb_free"][g % 2]
                k1 = P.op("pool", lambda e, h=h: e.tensor_tensor(out=ta, in0=h[0], in1=h[2], op=ALU.add), w0 + ([fstate.get("tmp_free")] if fstate.get("tmp_free") else []))
                k2 = P.op("pool", lambda e, h=h: e.tensor_tensor(out=tb_, in0=h[1], in1=h[3], op=ALU.add), w0 + ([fstate.get("tmp_free")] if fstate.get("tmp_free") else []))
                k3 = P.op("pool", lambda e, cm=cm: e.tensor_tensor(out=cm[:, 0, :], in0=ta, in1=tb_, op=ALU.add), [k1, k2])
                k4 = P.op("pool", lambda e, cm=cm: e.tensor_tensor(out=cm[:, 1, :], in0=ta, in1=tb_, op=ALU.subtract), [k1, k2])
                fstate["tmp_free"] = k4
                k5 = P.op("pool", lambda e, cm=cm, h=h: e.tensor_tensor(out=cm[:, 2, :], in0=h[0], in1=h[2], op=ALU.subtract), w0)
                k6 = P.op("pool", lambda e, cm=cm, h=h: e.tensor_tensor(out=cm[:, 3, :], in0=h[1], in1=h[3], op=ALU.subtract), w0)
                F2["cm_ready"][g] = [k3, k4, k5, k6]

            def chdft(g, r):
                cm = f["cmb"][g % 2]
                cm_ready = F2["cm_ready"][g]
                z = zp[r]
                z_ready = []
                for jp in range(4):
                    zb = F2["zcount"] % 4
                    F2["zcount"] += 1
                    bank = ps[ZB[zb]]
                    wz = cm_ready + wt + [F2["zb_free"][zb]]
                    mm = None
                    for jj in range(2):
                        j = jp * 2 + jj
                        pv = bank[:, jj * 256:(jj + 1) * 256]
                        sl = slice(j * 128, (j + 1) * 128)
                        if r == 0 or r == 2:
                            u = cm[:, 0 if r == 0 else 1, sl]
                            mm = P.op("pe", lambda e, u=u, pv=pv: e.matmul(pv, lhsT=u, rhs=rmat[:, 0:256], start=True, stop=True),
                                      wz, signal=(jj == 1))
                        else:
                            r2 = rmat[:, 256:512] if r == 1 else rmat[:, 512:768]
                            P.op("pe", lambda e, cm=cm, sl=sl, pv=pv: e.matmul(pv, lhsT=cm[:, 2, sl], rhs=rmat[:, 0:256],
                                                                               start=True, stop=False), wz, signal=False)
                            mm = P.op("pe", lambda e, cm=cm, sl=sl, pv=pv, r2=r2: e.matmul(pv, lhsT=cm[:, 3, sl], rhs=r2,
                                                                                           start=False, stop=True),
                                      wz, signal=(jj == 1))
                    wzp = [mm] + F2["zp_free"][r]
                    j0 = jp * 2
                    if r == 0:
                        src = bank[:].rearrange("p (j a b) -> p j a b", a=2, b=128)
                        if jp % 2 == 0:
                            ev = P.op("dve", lambda e, z=z, j0=j0, src=src: e.tensor_copy(out=z[:, j0:j0 + 2, :, :], in_=src), wzp)
                        else:
                            ev = P.op("act", lambda e, z=z, j0=j0, src=src: e.activation(out=z[:, j0:j0 + 2, :, :], in_=src, func=AF.Copy), wzp)
                        F2["zb_free"][zb] = ev
                        z_ready.append(ev)
                    else:
                        tt = f["t12"][F2["tcount"] % 2]
                        tfree = F2["t12_free"][F2["tcount"] % 2]
                        tslot = F2["tcount"] % 2
                        F2["tcount"] += 1
                        a_last = None
                        for jj in range(2):
                            j = j0 + jj
                            tss = tw[:, 32 + r * 8 + j:32 + r * 8 + j + 1]
                            zr = bank[:, jj * 256:jj * 256 + 128]
                            zi = bank[:, jj * 256 + 128:jj * 256 + 256]
                            P.op("act", lambda e, tt=tt, zi=zi, tss=tss, jj=jj: e.activation(out=tt[:, 2 * jj, :], in_=zi, func=AF.Copy, scale=tss),
                                 [mm, tfree] + wt, signal=False)
                            a_last = P.op("act", lambda e, tt=tt, zr=zr, tss=tss, jj=jj: e.activation(out=tt[:, 2 * jj + 1, :], in_=zr, func=AF.Copy, scale=tss),
                                          [mm, tfree] + wt, signal=(jj == 1))
                        d_last = None
                        for jj in range(2):
                            j = j0 + jj
                            tcs = tw[:, r * 8 + j:r * 8 + j + 1]
                            zr = bank[:, jj * 256:jj * 256 + 128]
                            zi = bank[:, jj * 256 + 128:jj * 256 + 256]
                            P.op("dve", lambda e, z=z, j=j, zr=zr, tcs=tcs, tt=tt, jj=jj: e.scalar_tensor_tensor(
                                out=z[:, j, 0, :], in0=zr, scalar=tcs, in1=tt[:, 2 * jj, :], op0=ALU.mult, op1=ALU.add),
                                [a_last] + wzp, signal=False)
                            d_last = P.op("dve", lambda e, z=z, j=j, zi=zi, tcs=tcs, tt=tt, jj=jj: e.scalar_tensor_tensor(
                                out=z[:, j, 1, :], in0=zi, scalar=tcs, in1=tt[:, 2 * jj + 1, :], op0=ALU.mult, op1=ALU.subtract),
                                [a_last] + wzp, signal=(jj == 1))
                        F2["t12_free"][tslot] = d_last
                        F2["zb_free"][zb] = d_last
                        z_ready.append(d_last)
                return z_ready

            def seqdft(g, r, z_ready):
                z = zp[r]
                cm_ready = F2["cm_ready"][g]
                mm = None
                for half in range(2):
                    fb = F2["fcount"] % 4
                    F2["fcount"] += 1
                    pv = ps[FB[fb]][:]
                    for j in range(8):
                        P.op("pe", lambda e, z=z, j=j, half=half, pv=pv: e.matmul(
                            pv, lhsT=z[:, j, 0, :], rhs=cq[:, j, half * 512:(half + 1) * 512], start=(j == 0), stop=False),
                            z_ready + wt + [F2["fb_free"][fb]], signal=False)
                        mm = P.op("pe", lambda e, z=z, j=j, half=half, pv=pv: e.matmul(
                            pv, lhsT=z[:, j, 1, :], rhs=sq[:, j, half * 512:(half + 1) * 512], start=False, stop=(j == 7)),
                            z_ready + wt, signal=(j == 7))
                    dst = hT[:, g, half * 2048:(half + 1) * 2048].rearrange("p (k r) -> p r k", r=4)[:, r, :]
                    if (F2["fcount"] % 2) == 0:
                        ev = P.op("act", lambda e, dst=dst, pv=pv: e.activation(out=dst, in_=pv, func=AF.Copy), [mm] + cm_ready)
                    else:
                        ev = P.op("dve", lambda e, dst=dst, pv=pv: e.tensor_copy(out=dst, in_=pv), [mm] + cm_ready)
                    F2["fb_free"][fb] = ev
                F2["zp_free"][r] = [mm]
                if r == 3:
                    F2["cmb_free"][g % 2] = [mm]

            units = [(g, r) for g in range(8) for r in range(4)]
            butterflies(0)
            butterflies(1)
            zr_pending = {0: chdft(0, 0)}
            for ui, (g, r) in enumerate(units):
                if ui + 1 < len(units):
                    g2, r2_ = units[ui + 1]
                    zr_pending[ui + 1] = chdft(g2, r2_)
                seqdft(g, r, zr_pending.pop(ui))
                if r == 3 and g + 2 < 8:
                    butterflies(g + 2)
            fT_all = [P.last("act"), P.last("dve")]

            ld3 = [None] * nblk
            st3 = [None] * nblk
            WB = (0, 1, 2, 3)
            wb_free = [None] * 4
            wc = 0

            def load3(i):
                w = [st3[i - 2]] if i >= 2 else [nrm[nblk - 2 + (i % 2)]]
                ld3[i] = P.dma("sp", "xl%d" % (i % 2), lambda e, i=i: e.dma_start(out=xb[i % 2], in_=rows(x_src, r0 + i * 256, 2)), w)

            load3(0)
            load3(1)
            for i in range(nblk):
                evs = []
                for t in range(2):
                    for half in range(2):
                        k = wc % 4
                        wc += 1
                        pv = ps[WB[k]][:]
                        mm = None
                        for g in range(8):
                            mm = P.op("pe", lambda e, g=g, t=t, half=half, pv=pv, i=i: e.matmul(
                                pv, lhsT=hT[:, g, i * 256 + t * 128:i * 256 + (t + 1) * 128],
                                rhs=wout[:, g, half * 512:(half + 1) * 512], start=(g == 0), stop=(g == 7)),
                                fT_all + wt + [wb_free[k]], signal=(g == 7))
                        xs = xb[i % 2][:, t, half * 512:(half + 1) * 512]
                        ev = P.op("dve", lambda e, xs=xs, pv=pv: e.tensor_tensor(out=xs, in0=pv, in1=xs, op=ALU.add), [mm, ld3[i]])
                        wb_free[k] = ev
                        evs.append(ev)
                st3[i] = P.dma("sp", "xs%d" % (i % 2), lambda e, i=i: e.dma_start(out=rows(y_d, r0 + i * 256, 2), in_=xb[i % 2]), evs)
                if i + 2 < nblk:
                    load3(i + 2)
            P.barrier()

        astate = {}

        def phase_attn_setup():
            A.reset(base_mark)
            a = astate
            a["wqkv"] = A.alloc(BF16, [8, 1536])
            a["wo"] = A.alloc(BF16, [8, 1024])
            a["ropec"] = A.alloc(F32, [32, 2, 32])
            a["ropes"] = A.alloc(F32, [32, 2, 32])
            a["ropen"] = A.alloc(F32, [32, 2, 32])
            a["qgb"] = A.alloc(F32, [128])
            a["kgb"] = A.alloc(F32, [128])
            a["gt"] = A.alloc(F32, [10, 128])
            a["kT"] = A.alloc(BF16, [2, 4096])
            a["v"] = A.alloc(BF16, [32, 256])
            a["mark"] = A.mark()
            wt = load_weight(a["wqkv"], wqkv_d, 4, "wq")
            a["wo_tok"] = None
            for nm, d_ in (("ropec", ropec_d), ("ropes", ropes_d), ("ropen", ropen_d)):
                wt.append(P.dma("pool", "wr", lambda e, nm=nm, d_=d_: e.dma_start(
                    out=a[nm], in_=d_[:, :].rearrange("p (t x i) -> p t x i", x=2, i=32))))
            wt.append(P.dma("pool", "wr", lambda e: e.dma_start(out=a["qgb"], in_=qg_d[0:1, :].partition_broadcast(128))))
            wt.append(P.dma("pool", "wr", lambda e: e.dma_start(out=a["kgb"], in_=kg_d[0:1, :].partition_broadcast(128))))
            g1 = P.op("dve", lambda e: e.tensor_scalar(out=a["gt"][:, 0:8, :], in0=a["qgb"].unsqueeze(1).broadcast_to([128, 8, 128]),
                                                       scalar1=float(128.0 ** -0.5), scalar2=None, op0=ALU.mult), wt)
            g2 = P.op("dve", lambda e: e.tensor_copy(out=a["gt"][:, 8:10, :], in_=a["kgb"].unsqueeze(1).broadcast_to([128, 2, 128])), wt)
            a["wt"] = wt + [g1, g2]
            a["wo_tok"] = load_weight(a["wo"], wo_d, 4, "wo")

        def phase_attn(sq_i):
            a = astate
            wt = a["wt"]
            wqkv, wo, kT, v, gt = a["wqkv"], a["wo"], a["kT"], a["v"], a["gt"]
            r0 = sq_i * S
            nblk = S // 256
            NSUB = nblk * 2
            A.reset(a["mark"])
            xb = [A.alloc(F32, [2, 1024]) for _ in range(2)]
            hn = A.alloc(BF16, [2, 1024])
            hTs = [A.alloc(BF16, [8, 256]) for _ in range(2)]
            st = [A.alloc(F32, [8]) for _ in range(2)]
            sqb = [A.alloc(F32, [10, 128]) for _ in range(2)]
            raw = [A.alloc(F32, [10, 128]) for _ in range(3)]
            qn = [A.alloc(F32, [10, 128]) for _ in range(3)]
            ra = A.alloc(F32, [10, 128])
            rb = A.alloc(F32, [10, 128])
            qr = [A.alloc(BF16, [10, 128]) for _ in range(3)]
            ss = [A.alloc(F32, [32]) for _ in range(3)]
            qTs = [A.alloc(BF16, [8, 256]) for _ in range(2)]
            TB = (0, 1)
            QB = (2, 3, 4, 5)
            HB = (6, 7)
            NQB = len(QB)
            ld = [None] * nblk
            nrm = [None] * nblk
            hTr = [None] * nblk
            S1 = dict(hn_free=[], hT_free=[[], []], tb_free=[], xfree=[None, None], qb_free=[None] * NQB, qcount=0,
                      sqb_free=[[], []], raw_free=[[], [], []], qn_free=[[], [], []], ra_free=[], rb_free=[], qr_free=[[], [], []],
                      hbq_free=[], hbk_free=[],
                      qTs_free=[None, None])
            fr = [None] * NSUB

            def load(i):
                ld[i] = P.dma("sp", "xl%d" % (i % 2), lambda e, i=i: e.dma_start(out=xb[i % 2], in_=rows(y_d, r0 + i * 256, 2)),
                              [S1["xfree"][i % 2]])

            def norm(i):
                nrm[i] = norm_block(xb[i % 2], hn, st[i % 2], [ld[i]], S1["hn_free"])
                S1["xfree"][i % 2] = nrm[i]

            def transp(i):
                pe_tok, ev = transpose_block(hn, hTs[i % 2], gA, TB, [nrm[i]] + t_c, S1["hT_free"][i % 2], S1["tb_free"])
                S1["hn_free"] = [pe_tok]
                S1["tb_free"] = ev
                hTr[i] = ev

            FR = {}

            def front_mm(s):
                i, t = s // 2, s % 2
                hT = hTs[i % 2]
                banks = []
                mms = []
                for n in range(3):
                    k = S1["qcount"] % NQB
                    S1["qcount"] += 1
                    pv = ps[QB[k]][:]
                    mm = None
                    for c in range(8):
                        mm = P.op("pe", lambda e, c=c, n=n, t=t, pv=pv, hT=hT: e.matmul(
                            pv, lhsT=hT[:, c, t * 128:(t + 1) * 128], rhs=wqkv[:, c, n * 512:(n + 1) * 512],
                            start=(c == 0), stop=(c == 7)), hTr[i] + wt + [S1["qb_free"][k]], signal=(c == 7))
                    banks.append((k, pv))
                    mms.append(mm)
                if t == 1:
                    S1["hT_free"][i % 2] = [mms[-1]]
                sq_ = sqb[s % 2]
                sq_t = []
                vcp = None
                for n in range(3):
                    k, pv = banks[n]
                    if n < 2:
                        sq_t.append(P.op("act", lambda e, n=n, pv=pv, sq_=sq_: e.activation(
                            out=sq_[:, n * 4:(n + 1) * 4, :], in_=pv.rearrange("p (h d) -> p h d", d=128), func=AF.Square),
                            [mms[n]] + S1["sqb_free"][s % 2]))
                    else:
                        sq_t.append(P.op("act", lambda e, pv=pv, sq_=sq_: e.activation(
                            out=sq_[:, 8:10, :], in_=pv[:, 0:256].rearrange("p (h d) -> p h d", d=128), func=AF.Square),
                            [mms[n]] + S1["sqb_free"][s % 2]))
                        vcp = P.op("act", lambda e, pv=pv, s=s: e.activation(
                            out=v[:, s, :], in_=pv[:, 256:512], func=AF.Copy), [mms[n]] + a.get("v_free", []))
                raw_ = raw[s % 3]
                cp_t = []
                for n in range(3):
                    k, pv = banks[n]
                    nh = 4 if n < 2 else 2
                    tk = P.op("act", lambda e, n=n, nh=nh, pv=pv, raw_=raw_: e.activation(
                        out=raw_[:, n * 4:n * 4 + nh, :], in_=pv[:, 0:nh * 128].rearrange("p (h d) -> p h d", d=128), func=AF.Copy),
                        [mms[n]] + S1["raw_free"][s % 3])
                    cp_t.append(tk)
                    S1["qb_free"][k] = tk
                FR[s] = dict(banks=banks, mms=mms, sq_t=sq_t, vcp=vcp, cp_t=cp_t)

            def front_norm(s):
                d = FR[s]
                banks, mms, sq_t, vcp, cp_t = d["banks"], d["mms"], d["sq_t"], d["vcp"], d["cp_t"]
                raw_ = raw[s % 3]
                sst = ss[s % 3]
                sq_ = sqb[s % 2]
                rd = P.op("dve", lambda e: e.tensor_reduce(out=sst[:, 0:10], in_=sq_, axis=AX.X, op=ALU.add), sq_t)
                S1["sqb_free"][s % 2] = [rd]
                rt = P.op("act", lambda e: e.activation(out=sst[:, 10:20], in_=sst[:, 0:10], func=AF.Sqrt,
                                                        scale=1.0 / 128.0, bias=epsc[:, 0:1]), [rd])
                rc = P.op("dve", lambda e: e.reciprocal(out=sst[:, 20:30], in_=sst[:, 10:20]), [rt])
                qn_ = qn[s % 3]
                qn_t = []
                for n in range(3):
                    k, pv = banks[n]
                    nh = 4 if n < 2 else 2
                    src = raw_[:, n * 4:n * 4 + nh, :]
                    rin = sst[:, 20 + n * 4:20 + n * 4 + nh].unsqueeze(2).broadcast_to([128, nh, 128])
                    tk = P.op("dve", lambda e, n=n, nh=nh, src=src, rin=rin: e.tensor_tensor(
                        out=qn_[:, n * 4:n * 4 + nh, :], in0=src, in1=rin, op=ALU.mult),
                        [rc, cp_t[n]] + S1["qn_free"][s % 3])
                    qn_t.append(tk)
                S1["raw_free"][s % 3] = qn_t
                d["qn_t"] = qn_t

            def gain(s):
                qn_ = qn[s % 3]
                FR[s]["g"] = P.op("pool", lambda e: e.tensor_tensor(out=qn_, in0=qn_, in1=gt, op=ALU.mult), FR[s]["qn_t"])

            def back_compute(s):
                g_ = FR[s]["g"]
                qn_ = qn[s % 3]
                qv = qn_.rearrange("p h (x f i) -> p h x f i", x=2, f=2)
                rav = ra.rearrange("p h (x f i) -> p h x f i", x=2, f=2)
                rbv = rb.rearrange("p h (x f i) -> p h x f i", x=2, f=2)
                cosb = a["ropec"][:, s, :, :].unsqueeze(1).broadcast_to([128, 10, 2, 32])
                sinb = a["ropes"][:, s, :, :].unsqueeze(1).broadcast_to([128, 10, 2, 32])
                nsinb = a["ropen"][:, s, :, :].unsqueeze(1).broadcast_to([128, 10, 2, 32])
                o1a = P.op("dve", lambda e: e.tensor_tensor(out=rav[:, :, :, 0, :], in0=qv[:, :, :, 0, :], in1=cosb, op=ALU.mult),
                           [g_] + S1["ra_free"])
                o1 = P.op("dve", lambda e: e.tensor_tensor(out=rav[:, :, :, 1, :], in0=qv[:, :, :, 1, :], in1=cosb, op=ALU.mult),
                          [g_] + S1["ra_free"])
                o2 = P.op("pool", lambda e: e.tensor_tensor(out=rbv[:, :, :, 0, :], in0=qv[:, :, :, 1, :], in1=nsinb, op=ALU.mult),
                          [g_] + S1["rb_free"])
                o3 = P.op("pool", lambda e: e.tensor_tensor(out=rbv[:, :, :, 1, :], in0=qv[:, :, :, 0, :], in1=sinb, op=ALU.mult),
                          [g_] + S1["rb_free"])
                S1["qn_free"][s % 3] = [o1a, o1, o2, o3]
                qrr = qr[s % 3]
                o4 = P.op("dve", lambda e: e.tensor_tensor(out=qrr, in0=ra, in1=rb, op=ALU.add), [o1a, o1, o2, o3] + S1["qr_free"][s % 3])
                S1["ra_free"] = [o4]
                S1["rb_free"] = [o4]
                FR[s]["o4"] = o4

            def back_pe(s):
                i, t = s // 2, s % 2
                o4 = FR[s]["o4"]
                qrr = qr[s % 3]
                pvq = psb(HB[0])
                pvk = psb(HB[1])
                tkh = None
                for hh in range(8):
                    tkh = P.op("pe", lambda e, hh=hh: e.transpose(out=pvq[:, hh * 128:(hh + 1) * 128], in_=qrr[:, hh, :], identity=ident),
                               [o4] + S1["hbq_free"], signal=(hh == 7))
                tkk = None
                for hh in range(2):
                    tkk = P.op("pe", lambda e, hh=hh: e.transpose(out=pvk[:, hh * 128:(hh + 1) * 128], in_=qrr[:, 8 + hh, :], identity=ident),
                               [o4] + S1["hbk_free"], signal=(hh == 1))
                S1["qr_free"][s % 3] = [tkk]
                qts = qTs[i % 2]
                e1 = P.op("act", lambda e: e.activation(
                    out=qts[:, :, t * 128:(t + 1) * 128], in_=pvq.rearrange("p (h k) -> p h k", k=128), func=AF.Copy),
                    [tkh, S1["qTs_free"][i % 2]])
                S1["hbq_free"] = [e1]
                e2 = P.op("act", lambda e: e.activation(
                    out=kT[:, :, s * 128:(s + 1) * 128], in_=pvk[:, 0:256].rearrange("p (h k) -> p h k", k=128), func=AF.Copy),
                    [tkk] + a.get("k_free", []))
                S1["hbk_free"] = [e2]
                del FR[s]
                return e1

            load(0)
            load(1)
            norm(0)
            transp(0)
            norm(1)
            transp(1)
            if nblk > 2:
                load(2)
                norm(2)
            front_mm(0)
            front_norm(0)
            front_mm(1)
            front_norm(1)
            gain(0)
            qst = [None] * nblk
            e_prev = {}
            for s in range(NSUB + 1):
                if s < NSUB:
                    back_compute(s)
                    if s + 1 < NSUB:
                        gain(s + 1)
                    if s + 2 < NSUB:
                        front_mm(s + 2)
                if s >= 1:
                    sp = s - 1
                    ip, tp = sp // 2, sp % 2
                    e1 = back_pe(sp)
                    if tp == 0:
                        e_prev[ip] = e1
                    else:
                        qb512, hf = ip // 2, ip % 2
                        qst[ip] = P.dma("sp", "qs%d" % (ip % 2), lambda e, ip=ip, qb512=qb512, hf=hf: e.dma_start(
                            out=qscr[qb512].rearrange("p (h k) -> p h k", k=512)[:, :, hf * 256:(hf + 1) * 256], in_=qTs[ip % 2]),
                            [e_prev[ip], e1])
                        S1["qTs_free"][ip % 2] = qst[ip]
                if s < NSUB:
                    if s + 2 < NSUB:
                        front_norm(s + 2)
                    i, t = s // 2, s % 2
                    if t == 1 and i + 2 < nblk:
                        transp(i + 2)
                        if i + 3 < nblk:
                            load(i + 3)
                            norm(i + 3)
            P.barrier()
            if "DBGQ" in phases:
                return

            A.reset(a["mark"])
            qT = [A.alloc(BF16, [8, 512]) for _ in range(2)]
            NPT = 6
            pT = [A.alloc(BF16, [512]) for _ in range(NPT)]
            oT = [A.alloc(BF16, [8, 512]) for _ in range(2)]
            rec = [A.alloc(F32, [512]) for _ in range(2)]
            accd = [A.alloc(F32, [512]) for _ in range(2)]
            tra = [A.alloc(BF16, [512]) for _ in range(2)]
            trb = [A.alloc(BF16, [512]) for _ in range(2)]
            trc = [A.alloc(BF16, [512]) for _ in range(2)]
            onesf = A.alloc(F32, [128])
            xr = [A.alloc(F32, [4, 1024]) for _ in range(2)]
            SB = (0, 1, 2, 3)
            OB = ((4, 5), (6, 7))
            NQ = S // 512
            S2 = dict(sb_free=[None] * 4, pT_free=[[] for _ in range(NPT)], scount=0, pcount=0, ob_free=[[None, None], [None, None]],
                      oT_free=[[], []], rec_free=[None, None], qT_free=[None, None], xr_free=[None, None], it=0,
                      acc_free=[[], []], tr_free=[], oT_ready=[[], []])
            pending_fin = []
            pending_wo = []
            pending_tail = []
            ldq = [None] * NQ
            ldx = [None] * NQ
            stx = [None] * NQ
            t_of = P.op("pool", lambda e: e.memset(onesf, 1.0))

            def loadqT(qb):
                ldq[qb] = P.dma("sp", "ql%d" % (qb % 2), lambda e, qb=qb: e.dma_start(
                    out=qT[qb % 2], in_=qscr[qb].rearrange("p (h k) -> p h k", k=512)), [S2["qT_free"][qb % 2]])

            def loadxr(qb):
                ldx[qb] = P.dma("sp", "xr%d" % (qb % 2), lambda e, qb=qb: e.dma_start(
                    out=xr[qb % 2], in_=rows(y_d, r0 + qb * 512, 4)), [S2["xr_free"][qb % 2]])

            def loadq(qb):
                loadqT(qb)
                loadxr(qb)

            loadq(0)
            if NQ > 1:
                loadq(1)
            EXS = {}

            def issue_s(qb_, h_, kc):
                g_ = h_ // 4
                qTb_ = qT[qb_ % 2]
                k = S2["scount"] % 4
                S2["scount"] += 1
                pk = S2["pcount"] % NPT
                S2["pcount"] += 1
                pv = ps[SB[k]][:]
                mm = P.op("pe", lambda e: e.matmul(
                    pv, lhsT=kT[:, g_, kc * 128:(kc + 1) * 128], rhs=qTb_[:, h_, :], start=True, stop=True),
                    [ldq[qb_], S2["sb_free"][k]])
                ptile = pT[pk]
                ex = P.op("act", lambda e: e.activation(out=ptile, in_=pv, func=AF.Exp),
                          [mm] + S2["pT_free"][pk])
                S2["sb_free"][k] = ex
                EXS[(qb_, h_, kc)] = (pk, ex, ptile)

            def tile_after(qb_, h_, kc, d):
                idx = (qb_ * 8 + h_) * 32 + kc + d
                if idx >= NQ * 8 * 32:
                    return None
                return (idx // 256, (idx // 32) % 8, idx % 32)

            for kc0 in range(3):
                issue_s(0, 0, kc0)
            for qb in range(NQ):
                qTb = qT[qb % 2]
                oTb = oT[qb % 2]
                for h in range(8):
                    g = h // 4
                    it = S2["it"]
                    S2["it"] += 1
                    ob_o, ob_s = OB[it % 2]
                    pvo = ps[ob_o][:]
                    pvs = ps[ob_s][:]
                    ad = accd[it % 2]
                    exs = {}
                    T = dict(a=[None, None], b=[None, None], c=[None, None], acc=None, accs=[])

                    def emit_c(m, T=T, it=it):
                        ta_, tb__, tc_ = tra[m % 2], trb[m % 2], trc[m % 2]
                        T["c"][m % 2] = P.op("dve", lambda e, ta_=ta_, tb__=tb__, tc_=tc_: e.tensor_tensor(out=tc_, in0=ta_, in1=tb__, op=ALU.add),
                                             [T["a"][m % 2], T["b"][m % 2], T["accs"][m - 2] if m >= 2 else None] + S2["tr_free"])

                    def emit_acc(m, T=T, it=it, pvs=pvs):
                        tc_ = trc[m % 2]
                        T["acc"] = P.op("pe", lambda e, tc_=tc_, pvs=pvs, m=m: e.matmul(pvs, lhsT=ones, rhs=tc_, start=(m == 0), stop=(m == 7)),
                                        [T["c"][m % 2], S2["ob_free"][it % 2][1]])
                        T["accs"].append(T["acc"])

                    def issue_pv(kc):
                        pk, ex, ptile = EXS[(qb, h, kc)]
                        mm = P.op("pe", lambda e, kc=kc, ptile=ptile, pvo=pvo, g=g: e.matmul(
                            pvo, lhsT=v[:, kc, g * 128:(g + 1) * 128], rhs=ptile, start=(kc == 0), stop=(kc == 31)),
                            [ex, S2["ob_free"][it % 2][0]], signal=True)
                        m, r4 = kc // 4, kc % 4
                        if r4 == 0 or r4 == 2:
                            S2["pT_free"][pk] = [mm]
                            EXS[(qb, h, kc)] = (pk, ex, ptile, mm)
                        else:
                            pk0, ex0, ptile0, mm0 = EXS[(qb, h, kc - 1)]
                            dst = (tra if r4 == 1 else trb)[m % 2]
                            sm = P.op("dve", lambda e, ptile=ptile, ptile0=ptile0, dst=dst: e.tensor_tensor(out=dst, in0=ptile0, in1=ptile, op=ALU.add),
                                      [ex, ex0, T["c"][m % 2] if m >= 2 else None] + S2["tr_free"])
                            (T["a"] if r4 == 1 else T["b"])[m % 2] = sm
                            S2["pT_free"][pk] = [mm, sm]
                            S2["pT_free"][pk0] = [mm0, sm]
                            if r4 == 1 and m >= 1:
                                emit_c(m - 1)
                            if r4 == 3 and m >= 1:
                                emit_acc(m - 1)
                        return mm

                    mm_last = None
                    for kc in range(32):
                        nt = tile_after(qb, h, kc, 3)
                        if nt is not None:
                            issue_s(*nt)
                        mm_last = issue_pv(kc)
                        if kc == 2 and pending_tail:
                            pending_tail.pop(0)()
                        if kc in (6, 8, 10, 12, 13) and pending_fin:
                            pending_fin.pop(0)()
                        if kc == 16 and pending_wo and not pending_fin:
                            pending_wo.pop(0)()
                    S2["tr_free"] = [T["c"][0], T["acc"]]

                    def tail(T=T, emit_c=emit_c, emit_acc=emit_acc):
                        emit_c(7)
                        emit_acc(7)
                        S2["tr_free"] = [T["c"][1], T["acc"]]
                    pending_tail.append(tail)

                    def make_fin(it=it, h=h, qb=qb, oTb=oTb, pvo=pvo, pvs=pvs, T=T, mm_last=mm_last):
                        rc_ = rec[it % 2]
                        st_ = dict(r=[])

                        def piece(j):
                            def run():
                                st_["r"].append(P.op("dve", lambda e: e.reciprocal(out=rc_[:, j * 128:(j + 1) * 128], in_=pvs[:, j * 128:(j + 1) * 128]),
                                                     [T["acc"], S2["rec_free"][it % 2]]))
                            return run

                        def mult():
                            d1 = st_["r"][-1]
                            d2 = P.op("dve", lambda e: e.tensor_tensor(out=oTb[:, h, :], in0=pvo, in1=rc_, op=ALU.mult),
                                      st_["r"] + [mm_last] + S2["oT_free"][qb % 2])
                            S2["rec_free"][it % 2] = d2
                            S2["ob_free"][it % 2] = [d2, d1]
                            S2["acc_free"][it % 2] = [T["acc"]]
                            S2["oT_ready"][qb % 2].append(d2)
                        return [piece(0), piece(1), piece(2), piece(3), mult]
                    pending_fin.extend(make_fin())
                S2["qT_free"][qb % 2] = P.last("pe")

                def make_wo_groups(qb=qb, oTb=oTb):
                    st_ = dict(evs=[], mm=None, rdy=None)

                    def group(gi):
                        def run():
                            if st_["rdy"] is None:
                                st_["rdy"] = S2["oT_ready"][qb % 2]
                                S2["oT_ready"][qb % 2] = []
                                if qb + 2 < NQ:
                                    loadqT(qb + 2)
                            t, half = gi // 2, gi % 2
                            k = S2["scount"] % 4
                            S2["scount"] += 1
                            pv = ps[SB[k]][:]
                            mm = None
                            for h in range(8):
                                mm = P.op("pe", lambda e, h=h: e.matmul(
                                    pv, lhsT=oTb[:, h, t * 128:(t + 1) * 128], rhs=wo[:, h, half * 512:(half + 1) * 512],
                                    start=(h == 0), stop=(h == 7)), st_["rdy"] + [S2["sb_free"][k]] + a["wo_tok"], signal=(h == 7))
                            xs = xr[qb % 2][:, t, half * 512:(half + 1) * 512]
                            ev = P.op("dve", lambda e: e.tensor_tensor(out=xs, in0=pv, in1=xs, op=ALU.add), [mm, ldx[qb]])
                            S2["sb_free"][k] = ev
                            st_["evs"].append(ev)
                            if gi == 7:
                                S2["oT_free"][qb % 2] = [mm]
                                stx[qb] = P.dma("sp", "xw%d" % (qb % 2), lambda e: e.dma_start(out=rows(y_d, r0 + qb * 512, 4), in_=xr[qb % 2]), st_["evs"])
                                S2["xr_free"][qb % 2] = stx[qb]
                                if qb + 2 < NQ:
                                    loadxr(qb + 2)
                        return run
                    return [group(gi) for gi in range(8)]
                pending_wo.extend(make_wo_groups())
            while pending_tail:
                pending_tail.pop(0)()
            while pending_fin:
                pending_fin.pop(0)()
            while pending_wo:
                pending_wo.pop(0)()
            a["v_free"] = [P.last("pe")]
            a["k_free"] = [P.last("pe")]
            P.barrier()

        if "C" in phases:
            cts = []
            for i in range(T // 256):
                cts.append(P.dma("sp", "cpy", lambda e, i=i: e.dma_start(out=y_d[i * 256:(i + 1) * 256, :], in_=x_d[i * 256:(i + 1) * 256, :])))
            P.barrier()
        if "F" in phases:
            phase_fourier_setup()
            for s_i in range(n_seq):
                phase_fourier(s_i, y_d if x_from_y else x_d)
        if "M0" in phases:
            phase_mlp(0, final=False)
        if "A" in phases:
            try:
                phase_attn_setup()
                chk(0)
                for s_i in range(n_seq):
                    phase_attn(s_i)
            except _Stop:
                pass
        if "M1" in phases:
            phase_mlp(1, final=True)
        P.barrier()

        block = es.enter_context(nc.Block())
        P.replay(block)
        _NC_CACHE["lastP"] = P
    return nc


_NC_CACHE = {}


def _common_inputs(fourier_norm, fourier_w_out, attn_norm, attn_w_qkv, attn_q_norm, attn_k_norm, attn_w_o,
                   mlp_norm, mlp_w_up, mlp_w_down, final_norm):
    f = lambda a: np.ascontiguousarray(np.asarray(a, dtype=np.float32))
    c = host_consts()
    m = dict(
        g_f=f(np.asarray(fourier_norm)[0].reshape(8, 128).T),
        g_a=f(np.asarray(attn_norm)[0].reshape(8, 128).T),
        g_m=f(np.concatenate([np.asarray(mlp_norm)[0].reshape(8, 128).T, np.asarray(mlp_norm)[1].reshape(8, 128).T], axis=1)),
        g_fin=f(np.asarray(final_norm).reshape(1, D)),
        qg=f(np.asarray(attn_q_norm).reshape(1, 128)),
        kg=f(np.asarray(attn_k_norm).reshape(1, 128)),
        w_fout=f(np.asarray(fourier_w_out)[0]),
        w_qkv=f(np.asarray(attn_w_qkv)[0]),
        w_o=f(np.asarray(attn_w_o)[0]),
        w_up=f(mlp_w_up),
        w_dn=f(mlp_w_down),
    )
    m.update(c)
    return m


def kernel(x_prompt, x_sample, fourier_norm, fourier_w_out, attn_norm, attn_w_qkv, attn_q_norm, attn_k_norm,
           attn_w_o, mlp_norm, mlp_w_up, mlp_w_down, final_norm):
    x_prompt = np.asarray(x_prompt, dtype=np.float32)
    x_sample = np.asarray(x_sample, dtype=np.float32)
    nb_p, nb_s = x_prompt.shape[0], x_sample.shape[0]
    x_all = np.concatenate([x_prompt.reshape(nb_p * S, D), x_sample.reshape(nb_s * S, D)], axis=0)
    common = _common_inputs(fourier_norm, fourier_w_out, attn_norm, attn_w_qkv, attn_q_norm, attn_k_norm, attn_w_o,
                            mlp_norm, mlp_w_up, mlp_w_down, final_norm)
    if "nc" not in _NC_CACHE:
        _NC_CACHE["nc"] = build_program()
    nc = _NC_CACHE["nc"]
    rows_per_core = SEQ_PER_CORE * S
    in_maps = []
    for c in range(NCORES):
        m = dict(common)
        m["x"] = x_all[c * rows_per_core:(c + 1) * rows_per_core]
        in_maps.append(m)
    res = run_bass_kernel_spmd(nc, in_maps, core_ids=list(range(NCORES)))
    y_all = np.concatenate([np.asarray(r["y"], dtype=np.float32) for r in res.results], axis=0)
    y_prompt = y_all[:nb_p * S].reshape(nb_p, S, D)
    y_sample = y_all[nb_p * S:].reshape(nb_s, S, D)
    return (y_prompt, y_sample)
```

```python
import numpy as np
from contextlib import ExitStack
import concourse.bass as bass
import concourse.mybir as mybir
from concourse.bass_utils import run_bass_kernel_spmd

F32 = mybir.dt.float32
BF16 = mybir.dt.bfloat16
AF = mybir.ActivationFunctionType
ALU = mybir.AluOpType
AX = mybir.AxisListType

D = 1024
S = 4096
NCORES = 8
SEQ_PER_CORE = 3
EPS = 1e-6
ENGS = ("sp", "act", "pool", "dve", "pe")
ARENA_WORDS = 52000


class _Stop(Exception):
    pass


def chk(n):
    return None


class Tok:
    __slots__ = ("sem", "val", "eng")

    def __init__(self, sem, val, eng):
        self.sem = sem
        self.val = val
        self.eng = eng


class Prog:
    def __init__(self, nc, es):
        self.nc = nc
        self.es = es
        self.q = {e: [] for e in ENGS}
        self.tl = {e: es.enter_context(nc.semaphore("tl_" + e)) for e in ("act", "pool", "dve", "pe")}
        self.tlc = {e: 0 for e in self.tl}
        self.dsem = {}
        self.dcnt = {}
        self.waited = {e: {} for e in ENGS}
        self.last_dma = {}

    def _waits(self, eng, waits):
        best = {}
        for t in waits:
            if t is None:
                continue
            if eng == "pe" and t.eng == "pe":
                continue
            k = id(t.sem)
            if k not in best or best[k].val < t.val:
                best[k] = t
        out = []
        for k, t in best.items():
            if self.waited[eng].get(k, 0) >= t.val:
                continue
            self.waited[eng][k] = t.val
            out.append((t.sem, t.val))
        return out

    def op(self, eng, fn, waits=(), signal=True):
        w = self._waits(eng, waits)
        tok = None
        inc = None
        if signal:
            self.tlc[eng] += 1
            tok = Tok(self.tl[eng], self.tlc[eng], eng)
            inc = (self.tl[eng], 1)
        self.q[eng].append((w, fn, inc))
        return tok

    def dma(self, eng, key, fn, waits=()):
        w = self._waits(eng, waits)
        if key not in self.dsem:
            self.dsem[key] = self.es.enter_context(self.nc.semaphore("d_" + key))
            self.dcnt[key] = 0
        sem = self.dsem[key]
        self.dcnt[key] += 16
        tok = Tok(sem, self.dcnt[key], "dma")
        self.q[eng].append((w, fn, (sem, 16)))
        self.last_dma[key] = tok
        return tok

    def wait_only(self, eng, waits):
        w = self._waits(eng, waits)
        if w:
            self.q[eng].append((w, None, None))

    def last(self, eng):
        if self.tlc[eng] == 0:
            return None
        return Tok(self.tl[eng], self.tlc[eng], eng)

    def barrier(self):
        toks = [self.last(e) for e in ("act", "pool", "dve", "pe")]
        toks += list(self.last_dma.values())
        for e in ENGS:
            self.wait_only(e, toks)

    def replay(self, block):
        def mk(engname):
            def body(e):
                for (w, fn, inc) in self.q[engname]:
                    for (sem, val) in w:
                        e.wait_ge(sem, val)
                    if fn is None:
                        continue
                    ins = fn(e)
                    if inc is not None:
                        ins.then_inc(inc[0], inc[1])
            return body
        block.sync(mk("sp"))
        block.scalar(mk("act"))
        block.gpsimd(mk("pool"))
        block.vector(mk("dve"))
        block.tensor(mk("pe"))


class Arena:
    def __init__(self, big, nwords):
        self.big = big
        self.n = nwords
        self.off = 0

    def mark(self):
        return self.off

    def reset(self, m):
        self.off = m

    def alloc(self, dt, shape):
        nel = int(np.prod(shape))
        nw = (nel * (4 if dt == F32 else 2) + 3) // 4
        nw = (nw + 7) // 8 * 8
        assert self.off + nw <= self.n, f"arena overflow {self.off}+{nw}>{self.n}"
        a = self.big[:, self.off:self.off + nw]
        self.off += nw
        if dt != F32:
            a = a.bitcast(dt)
        a = a[:, 0:nel]
        if len(shape) == 2:
            a = a.rearrange("p (a b) -> p a b", b=shape[1])
        elif len(shape) == 3:
            a = a.rearrange("p (a b c) -> p a b c", b=shape[1], c=shape[2])
        elif len(shape) == 4:
            a = a.rearrange("p (a b c d) -> p a b c d", b=shape[1], c=shape[2], d=shape[3])
        return a


_CONSTS = None


def host_consts():
    global _CONSTS
    if _CONSTS is not None:
        return _CONSTS
    Q = 1024
    s = np.arange(Q, dtype=np.int64)
    ang = 2.0 * np.pi * ((np.outer(s, s) % Q).astype(np.float64)) / Q
    cq = np.cos(ang).astype(np.float32)
    sq = np.sin(ang).astype(np.float32)
    c = np.arange(128, dtype=np.int64)
    angc = 2.0 * np.pi * ((np.outer(c, c) % 128).astype(np.float64)) / 128
    norm = 1.0 / np.sqrt(4096.0 * 128.0)
    Cc = np.cos(angc) * norm
    Sc = np.sin(angc) * norm
    rmat = np.concatenate([Cc, -Sc, -Sc, -Cc, Sc, Cc], axis=1).astype(np.float32)
    p = np.arange(128)[:, None, None]
    r = np.arange(4)[None, :, None]
    j = np.arange(8)[None, None, :]
    angt = 2.0 * np.pi * ((j * 128 + p) * r % 4096).astype(np.float64) / 4096.0
    tw = np.concatenate([np.cos(angt).reshape(128, 32), np.sin(angt).reshape(128, 32)], axis=1).astype(np.float32)
    inv = 10000.0 ** (-np.arange(0, 64, 2, dtype=np.float64) / 64.0)
    tok = (np.arange(32)[None, :] * 128 + np.arange(128)[:, None]).astype(np.float64)
    row = np.floor(tok / 64.0)
    col = tok - row * 64.0
    ang_r = (row.astype(np.float32)[:, :, None] * inv.astype(np.float32)[None, None, :]).astype(np.float32)
    ang_c = (col.astype(np.float32)[:, :, None] * inv.astype(np.float32)[None, None, :]).astype(np.float32)
    angs = np.stack([ang_r, ang_c], axis=2).astype(np.float64)
    ropec = np.cos(angs).astype(np.float32).reshape(128, 32 * 64)
    ropes = np.sin(angs).astype(np.float32).reshape(128, 32 * 64)
    ident = np.eye(128, dtype=np.float32)
    _CONSTS = dict(cq=cq, sq=sq, rmat=rmat, tw=tw, ropec=ropec, ropes=ropes, ropen=(-ropes).copy(), ident=ident)
    return _CONSTS


def build_program(n_seq=SEQ_PER_CORE, phases=("F", "M0", "A", "M1"), x_from_y=False, T_override=None):
    nc = bass.Bass("TRN2", target_bir_lowering=False)
    T = n_seq * S if T_override is None else T_override

    def din(name, shape, dt=F32):
        return nc.dram_tensor(name, list(shape), dt, kind="ExternalInput").ap()

    x_d = din("x", [T, D])
    y_d = nc.dram_tensor("y", [T, D], F32, kind="ExternalOutput").ap()
    gF_d = din("g_f", [128, 8])
    gA_d = din("g_a", [128, 8])
    gM_d = din("g_m", [128, 16])
    gfin_d = din("g_fin", [1, D])
    qg_d = din("qg", [1, 128])
    kg_d = din("kg", [1, 128])
    wout_d = din("w_fout", [D, D])
    wqkv_d = din("w_qkv", [D, 1536])
    wo_d = din("w_o", [D, D])
    wup_d = din("w_up", [2, D, 4096])
    wdn_d = din("w_dn", [2, 4096, D])
    cq_d = din("cq", [1024, 1024])
    sq_d = din("sq", [1024, 1024])
    rmat_d = din("rmat", [128, 768])
    tw_d = din("tw", [128, 64])
    ropec_d = din("ropec", [128, 2048])
    ropes_d = din("ropes", [128, 2048])
    ropen_d = din("ropen", [128, 2048])
    ident_d = din("ident", [128, 128])
    qscr = nc.dram_tensor("qscr", [8, 128, 8 * 512], BF16).ap()

    with ExitStack() as es:
        big = es.enter_context(nc.sbuf_tensor("arena", [128, ARENA_WORDS], F32))
        ps = [es.enter_context(nc.psum_tensor(f"ps{i}", [128, 512], F32)) for i in range(8)]
        P = Prog(nc, es)
        A = Arena(big, ARENA_WORDS)

        ident = A.alloc(BF16, [128])
        ones = A.alloc(BF16, [128])
        epsc = A.alloc(F32, [8])
        gF = A.alloc(F32, [8])
        gA = A.alloc(F32, [8])
        gM = A.alloc(F32, [16])
        t_c = [
            P.dma("pool", "cst", lambda e: e.dma_start(out=ident, in_=ident_d[:, :])),
            P.dma("pool", "cst", lambda e: e.dma_start(out=gF, in_=gF_d[:, :])),
            P.dma("pool", "cst", lambda e: e.dma_start(out=gA, in_=gA_d[:, :])),
            P.dma("pool", "cst", lambda e: e.dma_start(out=gM, in_=gM_d[:, :])),
        ]
        t_c.append(P.op("pool", lambda e: e.memset(ones, 1.0)))
        t_c.append(P.op("pool", lambda e: e.memset(epsc, EPS)))
        P.barrier()
        base_mark = A.mark()

        def psb(i):
            return ps[i][:].bitcast(BF16)

        def rows(ap_d, r0, nt):
            return ap_d[r0:r0 + nt * 128, :].rearrange("(t p) d -> p t d", p=128)

        def norm_block(xb, hn, st, waits, hn_free):
            t_sq = None
            for t in range(2):
                t_sq = P.op("act", lambda e, t=t: e.activation(out=hn[:, t, :], in_=xb[:, t, :], func=AF.Square,
                                                               accum_out=st[:, t:t + 1]), list(waits) + list(hn_free))
            t_rt = P.op("act", lambda e: e.activation(out=st[:, 2:4], in_=st[:, 0:2], func=AF.Sqrt,
                                                      scale=1.0 / D, bias=epsc[:, 0:1]), [t_sq])
            t_rc = P.op("dve", lambda e: e.reciprocal(out=st[:, 4:6], in_=st[:, 2:4]), [t_rt])
            toks = []
            for t in range(2):
                toks.append(P.op("dve", lambda e, t=t: e.tensor_scalar(out=hn[:, t, :], in0=xb[:, t, :],
                                                                       scalar1=st[:, 4 + t:5 + t], scalar2=None,
                                                                       op0=ALU.mult), [t_rc, t_sq]))
            return toks[-1]

        def transpose_block(hn, hT, gain, banks, waits, hT_free, bank_free):
            pe_tok = None
            for c in range(8):
                pv = psb(banks[c // 4])
                for t in range(2):
                    last = (c % 4 == 3 and t == 1)
                    tk = P.op("pe", lambda e, c=c, t=t, pv=pv: e.transpose(
                        out=pv[:, (c % 4) * 256 + t * 128:(c % 4) * 256 + t * 128 + 128],
                        in_=hn[:, t, c * 128:(c + 1) * 128], identity=ident),
                        list(waits) + list(bank_free), signal=last)
                    if last:
                        pe_tok = tk
                        if c == 3:
                            pe_tok_a = tk
            ev = []
            for c in range(8):
                pv = psb(banks[c // 4])
                src = pv[:, (c % 4) * 256:(c % 4) * 256 + 256]
                w = [pe_tok_a if c < 4 else pe_tok] + list(hT_free)
                if c < 4:
                    ev.append(P.op("act", lambda e, c=c, src=src: e.activation(out=hT[:, c, :], in_=src, func=AF.Copy,
                                                                               scale=gain[:, c:c + 1]), w))
                else:
                    ev.append(P.op("dve", lambda e, c=c, src=src: e.tensor_scalar(out=hT[:, c, :], in0=src,
                                                                                  scalar1=gain[:, c:c + 1], scalar2=None,
                                                                                  op0=ALU.mult), w))
            return pe_tok, ev

        def load_weight(dst, src_ap, nsplit, key):
            nch = dst.shape[1]
            per = nch // nsplit
            toks = []
            for i in range(nsplit):
                toks.append(P.dma("pool", key, lambda e, i=i: e.dma_start(
                    out=dst[:, i * per:(i + 1) * per, :],
                    in_=src_ap[i * per * 128:(i + 1) * per * 128, :].rearrange("(c p) f -> p c f", p=128))))
            return toks

        def phase_mlp(L, final):
            A.reset(base_mark)
            wup = A.alloc(BF16, [8, 4096])
            wdn = A.alloc(BF16, [32, 1024])
            xb = [A.alloc(F32, [2, 1024]) for _ in range(3)]
            hn = A.alloc(BF16, [2, 1024])
            hT = A.alloc(BF16, [8, 256])
            aT = A.alloc(BF16, [32, 256])
            rl = [A.alloc(F32, [512]) for _ in range(3)]
            st = [A.alloc(F32, [8]) for _ in range(2)]
            st2 = A.alloc(F32, [8])
            junk = A.alloc(BF16, [1024])
            gfin = A.alloc(F32, [1024]) if final else None
            gain = gM[:, L * 8:(L + 1) * 8]
            wu_tok = []
            for j in range(4):
                wu_tok.append(P.dma("pool", "wu%d" % j, lambda e, j=j: e.dma_start(
                    out=wup[:, :, j * 1024:(j + 1) * 1024],
                    in_=wup_d[L].rearrange("(c p) f -> p c f", p=128)[:, :, j * 1024:(j + 1) * 1024])))
            wd_tok = []
            for hf in range(2):
                wd_tok.append(P.dma("pool", "wd%d" % hf, lambda e, hf=hf: e.dma_start(
                    out=wdn[:, :, hf * 512:(hf + 1) * 512],
                    in_=wdn_d[L].rearrange("(c p) f -> p c f", p=128)[:, :, hf * 512:(hf + 1) * 512])))
            wt = []
            if final:
                wt.append(P.dma("pool", "wg", lambda e: e.dma_start(out=gfin, in_=gfin_d[0:1, :].partition_broadcast(128))))
            nblk = T // 256
            TB = (0, 1)
            UB = (2, 3, 4)
            DB = (5, 6, 7)
            ld = [None] * nblk
            stt = [None] * nblk
            nrm = [None] * nblk
            hTr = [None] * nblk
            state = dict(hn_free=[], hT_free=[], tb_free=[], up_last=None, dn_last=None,
                         up_slot_free=[None] * 3, rl_free=[None] * 3, db_free=[None] * 3, dcount=0, ucount=0)

            def load(i):
                w = [stt[i - 3]] if i >= 3 else []
                ld[i] = P.dma("sp", "xl%d" % (i % 3), lambda e, i=i: e.dma_start(out=xb[i % 3], in_=rows(y_d, i * 256, 2)), w)

            def norm(i):
                nrm[i] = norm_block(xb[i % 3], hn, st[i % 2], [ld[i]], state["hn_free"])

            def transp(i):
                pe_tok, ev = transpose_block(hn, hT, gain, TB, [nrm[i]], state["hT_free"], state["tb_free"])
                state["hn_free"] = [pe_tok]
                state["tb_free"] = ev
                hTr[i] = ev

            def up(i):
                aT_free = [state["dn_last"]]
                sq_toks = []
                for fp in range(16):
                    slot = state["ucount"] % 3
                    state["ucount"] += 1
                    bank = ps[UB[slot]]
                    mm = None
                    for f2 in range(2):
                        fc = fp * 2 + f2
                        pv = bank[:, f2 * 256:(f2 + 1) * 256]
                        for c in range(8):
                            mm = P.op("pe", lambda e, c=c, fc=fc, pv=pv: e.matmul(pv, lhsT=wup[:, c, fc * 128:(fc + 1) * 128],
                                                                                  rhs=hT[:, c, :], start=(c == 0), stop=(c == 7)),
                                      hTr[i] + [wu_tok[fc // 8], state["up_slot_free"][slot]], signal=(c == 7 and f2 == 1))
                    r = P.op("act", lambda e, bank=bank, slot=slot: e.activation(out=rl[slot], in_=bank[:], func=AF.Relu),
                             [mm, state["rl_free"][slot]])
                    state["up_slot_free"][slot] = r
                    q = P.op("dve", lambda e, fp=fp, slot=slot: e.tensor_tensor(
                        out=aT[:, fp * 2:fp * 2 + 2, :].rearrange("p a b -> p (a b)"), in0=rl[slot], in1=rl[slot],
                        op=ALU.mult), [r] + aT_free)
                    state["rl_free"][slot] = q
                    sq_toks.append(q)
                    sq_toks.append(q)
                state["hT_free"] = [mm]
                return sq_toks

            def down_group(i, g, sq_toks):
                t, half = g // 2, g % 2
                k = state["dcount"] % 3
                state["dcount"] += 1
                pv = ps[DB[k]][:]
                mm = None
                for fc in range(32):
                    mm = P.op("pe", lambda e, fc=fc, t=t, half=half, pv=pv: e.matmul(
                        pv, lhsT=aT[:, fc, t * 128:(t + 1) * 128], rhs=wdn[:, fc, half * 512:(half + 1) * 512],
                        start=(fc == 0), stop=(fc == 31)), [sq_toks[fc], state["db_free"][k], wd_tok[half]], signal=(fc == 31))
                state["dn_last"] = mm
                xs = xb[i % 3][:, t, half * 512:(half + 1) * 512]
                ev = P.op("dve", lambda e, xs=xs, pv=pv: e.tensor_tensor(out=xs, in0=pv, in1=xs, op=ALU.add), [mm])
                state["db_free"][k] = ev
                return ev

            def finish_sub(i, t, evs):
                if not final:
                    return evs
                xs = xb[i % 3][:, t, :]
                a = P.op("act", lambda e: e.activation(out=junk, in_=xs, func=AF.Square, accum_out=st2[:, t:t + 1]), evs)
                b = P.op("act", lambda e: e.activation(out=st2[:, 2 + t:3 + t], in_=st2[:, t:t + 1], func=AF.Sqrt,
                                                       scale=1.0 / D, bias=epsc[:, 0:1]), [a])
                c = P.op("dve", lambda e: e.reciprocal(out=st2[:, 4 + t:5 + t], in_=st2[:, 2 + t:3 + t]), [b])
                d = P.op("dve", lambda e: e.scalar_tensor_tensor(out=xs, in0=xs, scalar=st2[:, 4 + t:5 + t], in1=gfin,
                                                                 op0=ALU.mult, op1=ALU.mult), [c, a] + wt)
                return [d]

            load(0)
            if nblk > 1:
                load(1)
            norm(0)
            transp(0)
            if nblk > 1:
                norm(1)
            for i in range(nblk):
                if i + 2 < nblk:
                    load(i + 2)
                sq_toks = up(i)
                e0 = down_group(i, 0, sq_toks)
                if i + 1 < nblk:
                    transp(i + 1)
                e1 = down_group(i, 1, sq_toks)
                f0 = finish_sub(i, 0, [e0, e1])
                e2 = down_group(i, 2, sq_toks)
                e3 = down_group(i, 3, sq_toks)
                f1 = finish_sub(i, 1, [e2, e3])
                stt[i] = P.dma("sp", "xs%d" % (i % 3), lambda e, i=i: e.dma_start(out=rows(y_d, i * 256, 2), in_=xb[i % 3]),
                               f0 + f1)
                if i + 2 < nblk:
                    norm(i + 2)
            P.barrier()

        fstate = {}

        def phase_fourier_setup():
            A.reset(base_mark)
            f = fstate
            f["cq"] = A.alloc(BF16, [8, 1024])
            f["sq"] = A.alloc(BF16, [8, 1024])
            f["wout"] = A.alloc(BF16, [8, 1024])
            f["rmat"] = A.alloc(BF16, [768])
            f["tw"] = A.alloc(F32, [64])
            f["hT"] = A.alloc(BF16, [8, 4096])
            f["xb"] = [A.alloc(F32, [2, 1024]) for _ in range(2)]
            f["hn"] = A.alloc(BF16, [2, 1024])
            f["hn2"] = A.alloc(BF16, [2, 1024])
            f["hTb"] = A.alloc(BF16, [8, 256])
            f["st"] = [A.alloc(F32, [8]) for _ in range(2)]
            f["cmb"] = [A.alloc(BF16, [4, 1024]) for _ in range(2)]
            f["tmpa"] = A.alloc(F32, [1024])
            f["tmpb"] = A.alloc(F32, [1024])
            f["zp"] = [A.alloc(BF16, [8, 2, 128]) for _ in range(4)]
            f["t12"] = [A.alloc(F32, [4, 128]) for _ in range(2)]
            wt = load_weight(f["cq"], cq_d, 4, "w")
            wt += load_weight(f["sq"], sq_d, 4, "w")
            wt += load_weight(f["wout"], wout_d, 4, "w")
            wt.append(P.dma("pool", "w", lambda e: e.dma_start(out=f["rmat"], in_=rmat_d[:, :])))
            wt.append(P.dma("pool", "w", lambda e: e.dma_start(out=f["tw"], in_=tw_d[:, :])))
            f["wt"] = wt

        def phase_fourier(sq_i, x_src):
            f = fstate
            wt = f["wt"]
            hT, hn, hTb, cq, sq, wout, rmat, tw = f["hT"], f["hn"], f["hTb"], f["cq"], f["sq"], f["wout"], f["rmat"], f["tw"]
            xb = f["xb"]
            r0 = sq_i * S
            nblk = S // 256
            TB = (0, 1)
            ld = [None] * nblk
            nrm = [None] * nblk
            cp = [None] * nblk
            st_ = dict(hn_free=[[], []], tb_free=[[], []], xfree=[None, None])
            hns = [hn, f["hn2"]]
            TBs = [(0, 1), (2, 3)]

            def load(i):
                ld[i] = P.dma("sp", "xl%d" % (i % 2), lambda e, i=i: e.dma_start(out=xb[i % 2], in_=rows(x_src, r0 + i * 256, 2)),
                              [st_["xfree"][i % 2]])

            def norm(i):
                nrm[i] = norm_block(xb[i % 2], hns[i % 2], f["st"][i % 2], [ld[i]], st_["hn_free"][i % 2])
                st_["xfree"][i % 2] = nrm[i]

            load(0)
            load(1)
            norm(0)
            for i in range(nblk):
                if i + 1 < nblk:
                    norm(i + 1)
                hn_i = hns[i % 2]
                TBi = TBs[i % 2]
                pe_tok = None
                pe_tok_a = None
                for c in range(8):
                    pv = psb(TBi[c // 4])
                    for t in range(2):
                        lastm = (c % 4 == 3 and t == 1)
                        tk = P.op("pe", lambda e, c=c, t=t, pv=pv, hn_i=hn_i: e.transpose(
                            out=pv[:, (c % 4) * 256 + t * 128:(c % 4) * 256 + t * 128 + 128],
                            in_=hn_i[:, t, c * 128:(c + 1) * 128], identity=ident),
                            [nrm[i]] + st_["tb_free"][i % 2] + t_c, signal=lastm)
                        if lastm:
                            pe_tok = tk
                            if c == 3:
                                pe_tok_a = tk
                st_["hn_free"][i % 2] = [pe_tok]
                ev = []
                for c in range(8):
                    pv = psb(TBi[c // 4])
                    src = pv[:, (c % 4) * 256:(c % 4) * 256 + 256]
                    w = [pe_tok_a if c < 4 else pe_tok]
                    dst = hT[:, c, i * 256:(i + 1) * 256]
                    if c < 4:
                        ev.append(P.op("act", lambda e, c=c, src=src, dst=dst: e.activation(out=dst, in_=src, func=AF.Copy,
                                                                                            scale=gF[:, c:c + 1]), w))
                    else:
                        ev.append(P.op("dve", lambda e, c=c, src=src, dst=dst: e.tensor_scalar(out=dst, in0=src,
                                                                                               scalar1=gF[:, c:c + 1], scalar2=None,
                                                                                               op0=ALU.mult), w))
                st_["tb_free"][i % 2] = ev
                if i + 2 < nblk:
                    load(i + 2)
            hT_done = [P.last("act"), P.last("dve")]

            ZB = (0, 1, 2, 3)
            FB = (4, 5, 6, 7)
            zp = f["zp"]
            F2 = dict(zp_free=[[] for _ in range(4)], zb_free=[None] * 4, fb_free=[None] * 4, cmb_free=[[], []],
                      fcount=0, zcount=0, tcount=0, t12_free=[None, None], cm_ready={}, last_seq={})

            def butterflies(g):
                cm = f["cmb"][g % 2]
                ta, tb_ = f["tmpa"], f["tmpb"]
                h = [hT[:, g, q * 1024:(q + 1) * 1024] for q in range(4)]
                w0 = hT_done + F2["cmb_free"][g % 2]
                k1 = P.op("pool", lambda e, h=h: e.tensor_tensor(out=ta, in0=h[0], in1=h[2], op=ALU.add), w0 + ([fstate.get("tmp_free")] if fstate.get("tmp_free") else []))
                k2 = P.op("pool", lambda e, h=h: e.tensor_tensor(out=tb_, in0=h[1], in1=h[3], op=ALU.add), w0 + ([fstate.get("tmp_free")] if fstate.get("tmp_free") else []))
                k3 = P.op("pool", lambda e, cm=cm: e.tensor_tensor(out=cm[:, 0, :], in0=ta, in1=tb_, op=ALU.add), [k1, k2])
                k4 = P.op("pool", lambda e, cm=cm: e.tensor_tensor(out=cm[:, 1, :], in0=ta, in1=tb_, op=ALU.subtract), [k1, k2])
                fstate["tmp_free"] = k4
                k5 = P.op("pool", lambda e, cm=cm, h=h: e.tensor_tensor(out=cm[:, 2, :], in0=h[0], in1=h[2], op=ALU.subtract), w0)
                k6 = P.op("pool", lambda e, cm=cm, h=h: e.tensor_tensor(out=cm[:, 3, :], in0=h[1], in1=h[3], op=ALU.subtract), w0)
                F2["cm_ready"][g] = [k3, k4, k5, k6]

            def chdft(g, r):
                cm = f["cmb"][g % 2]
                cm_ready = F2["cm_ready"][g]
                z = zp[r]
                z_ready = []
                for jp in range(4):
                    zb = F2["zcount"] % 4
                    F2["zcount"] += 1
                    bank = ps[ZB[zb]]
                    wz = cm_ready + wt + [F2["zb_free"][zb]]
                    mm = None
                    for jj in range(2):
                        j = jp * 2 + jj
                        pv = bank[:, jj * 256:(jj + 1) * 256]
                        sl = slice(j * 128, (j + 1) * 128)
                        if r == 0 or r == 2:
                            u = cm[:, 0 if r == 0 else 1, sl]
                            mm = P.op("pe", lambda e, u=u, pv=pv: e.matmul(pv, lhsT=u, rhs=rmat[:, 0:256], start=True, stop=True),
                                      wz, signal=(jj == 1))
                        else:
                            r2 = rmat[:, 256:512] if r == 1 else rmat[:, 512:768]
                            P.op("pe", lambda e, cm=cm, sl=sl, pv=pv: e.matmul(pv, lhsT=cm[:, 2, sl], rhs=rmat[:, 0:256],
                                                                               start=True, stop=False), wz, signal=False)
                            mm = P.op("pe", lambda e, cm=cm, sl=sl, pv=pv, r2=r2: e.matmul(pv, lhsT=cm[:, 3, sl], rhs=r2,
                                                                                           start=False, stop=True),
                                      wz, signal=(jj == 1))
                    wzp = [mm] + F2["zp_free"][r]
                    j0 = jp * 2
                    if r == 0:
                        src = bank[:].rearrange("p (j a b) -> p j a b", a=2, b=128)
                        if jp % 2 == 0:
                            ev = P.op("dve", lambda e, z=z, j0=j0, src=src: e.tensor_copy(out=z[:, j0:j0 + 2, :, :], in_=src), wzp)
                        else:
                            ev = P.op("act", lambda e, z=z, j0=j0, src=src: e.activation(out=z[:, j0:j0 + 2, :, :], in_=src, func=AF.Copy), wzp)
                        F2["zb_free"][zb] = ev
                        z_ready.append(ev)
                    else:
                        tt = f["t12"][F2["tcount"] % 2]
                        tfree = F2["t12_free"][F2["tcount"] % 2]
                        tslot = F2["tcount"] % 2
                        F2["tcount"] += 1
                        a_last = None
                        for jj in range(2):
                            j = j0 + jj
                            tss = tw[:, 32 + r * 8 + j:32 + r * 8 + j + 1]
                            zr = bank[:, jj * 256:jj * 256 + 128]
                            zi = bank[:, jj * 256 + 128:jj * 256 + 256]
                            P.op("act", lambda e, tt=tt, zi=zi, tss=tss, jj=jj: e.activation(out=tt[:, 2 * jj, :], in_=zi, func=AF.Copy, scale=tss),
                                 [mm, tfree] + wt, signal=False)
                            a_last = P.op("act", lambda e, tt=tt, zr=zr, tss=tss, jj=jj: e.activation(out=tt[:, 2 * jj + 1, :], in_=zr, func=AF.Copy, scale=tss),
                                          [mm, tfree] + wt, signal=(jj == 1))
                        d_last = None
                        for jj in range(2):
                            j = j0 + jj
                            tcs = tw[:, r * 8 + j:r * 8 + j + 1]
                            zr = bank[:, jj * 256:jj * 256 + 128]
                            zi = bank[:, jj * 256 + 128:jj * 256 + 256]
                            P.op("dve", lambda e, z=z, j=j, zr=zr, tcs=tcs, tt=tt, jj=jj: e.scalar_tensor_tensor(
                                out=z[:, j, 0, :], in0=zr, scalar=tcs, in1=tt[:, 2 * jj, :], op0=ALU.mult, op1=ALU.add),
                                [a_last] + wzp, signal=False)
                            d_last = P.op("dve", lambda e, z=z, j=j, zi=zi, tcs=tcs, tt=tt, jj=jj: e.scalar_tensor_tensor(
                                out=z[:, j, 1, :], in0=zi, scalar=tcs, in1=tt[:, 2 * jj + 1, :], op0=ALU.mult, op1=ALU.subtract),
                                [a_last] + wzp, signal=(jj == 1))
                        F2["t12_free"][tslot] = d_last
                        F2["zb_free"][zb] = d_last
                        z_ready.append(d_last)
                return z_ready

            def seqdft(g, r, z_ready):
                z = zp[r]
                cm_ready = F2["cm_ready"][g]
                mm = None
                for half in range(2):
                    fb = F2["fcount"] % 4
                    F2["fcount"] += 1
                    pv = ps[FB[fb]][:]
                    for j in range(8):
                        P.op("pe", lambda e, z=z, j=j, half=half, pv=pv: e.matmul(
                            pv, lhsT=z[:, j, 0, :], rhs=cq[:, j, half * 512:(half + 1) * 512], start=(j == 0), stop=False),
                            z_ready + wt + [F2["fb_free"][fb]], signal=False)
                        mm = P.op("pe", lambda e, z=z, j=j, half=half, pv=pv: e.matmul(
                            pv, lhsT=z[:, j, 1, :], rhs=sq[:, j, half * 512:(half + 1) * 512], start=False, stop=(j == 7)),
                            z_ready + wt, signal=(j == 7))
                    dst = hT[:, g, half * 2048:(half + 1) * 2048].rearrange("p (k r) -> p r k", r=4)[:, r, :]
                    if (F2["fcount"] % 2) == 0:
                        ev = P.op("act", lambda e, dst=dst, pv=pv: e.activation(out=dst, in_=pv, func=AF.Copy), [mm] + cm_ready)
                    else:
                        ev = P.op("dve", lambda e, dst=dst, pv=pv: e.tensor_copy(out=dst, in_=pv), [mm] + cm_ready)
                    F2["fb_free"][fb] = ev
                F2["zp_free"][r] = [mm]
                if r == 3:
                    F2["cmb_free"][g % 2] = [mm]

            units = [(g, r) for g in range(8) for r in range(4)]
            butterflies(0)
            butterflies(1)
            zr_pending = {0: chdft(0, 0)}
            for ui, (g, r) in enumerate(units):
                if ui + 1 < len(units):
                    g2, r2_ = units[ui + 1]
                    zr_pending[ui + 1] = chdft(g2, r2_)
                seqdft(g, r, zr_pending.pop(ui))
                if r == 3 and g + 2 < 8:
                    butterflies(g + 2)
            fT_all = [P.last("act"), P.last("dve")]

            ld3 = [None] * nblk
            st3 = [None] * nblk
            WB = (0, 1, 2, 3)
            wb_free = [None] * 4
            wc = 0

            def load3(i):
                w = [st3[i - 2]] if i >= 2 else [nrm[nblk - 2 + (i % 2)]]
                ld3[i] = P.dma("sp", "xl%d" % (i % 2), lambda e, i=i: e.dma_start(out=xb[i % 2], in_=rows(x_src, r0 + i * 256, 2)), w)

            load3(0)
            load3(1)
            for i in range(nblk):
                evs = []
                for t in range(2):
                    for half in range(2):
                        k = wc % 4
                        wc += 1
                        pv = ps[WB[k]][:]
                        mm = None
                        for g in range(8):
                            mm = P.op("pe", lambda e, g=g, t=t, half=half, pv=pv, i=i: e.matmul(
                                pv, lhsT=hT[:, g, i * 256 + t * 128:i * 256 + (t + 1) * 128],
                                rhs=wout[:, g, half * 512:(half + 1) * 512], start=(g == 0), stop=(g == 7)),
                                fT_all + wt + [wb_free[k]], signal=(g == 7))
                        xs = xb[i % 2][:, t, half * 512:(half + 1) * 512]
                        ev = P.op("dve", lambda e, xs=xs, pv=pv: e.tensor_tensor(out=xs, in0=pv, in1=xs, op=ALU.add), [mm, ld3[i]])
                        wb_free[k] = ev
                        evs.append(ev)
                st3[i] = P.dma("sp", "xs%d" % (i % 2), lambda e, i=i: e.dma_start(out=rows(y_d, r0 + i * 256, 2), in_=xb[i % 2]), evs)
                if i + 2 < nblk:
                    load3(i + 2)
            P.barrier()

        astate = {}

        def phase_attn_setup():
            A.reset(base_mark)
            a = astate
            a["wqkv"] = A.alloc(BF16, [8, 1536])
            a["wo"] = A.alloc(BF16, [8, 1024])
            a["ropec"] = A.alloc(F32, [32, 2, 32])
            a["ropes"] = A.alloc(F32, [32, 2, 32])
            a["ropen"] = A.alloc(F32, [32, 2, 32])
            a["qgb"] = A.alloc(F32, [128])
            a["kgb"] = A.alloc(F32, [128])
            a["gt"] = A.alloc(F32, [10, 128])
            a["kT"] = A.alloc(BF16, [2, 4096])
            a["v"] = A.alloc(BF16, [32, 256])
            a["mark"] = A.mark()
            wt = load_weight(a["wqkv"], wqkv_d, 4, "wq")
            a["wo_tok"] = None
            for nm, d_ in (("ropec", ropec_d), ("ropes", ropes_d), ("ropen", ropen_d)):
                wt.append(P.dma("pool", "wr", lambda e, nm=nm, d_=d_: e.dma_start(
                    out=a[nm], in_=d_[:, :].rearrange("p (t x i) -> p t x i", x=2, i=32))))
            wt.append(P.dma("pool", "wr", lambda e: e.dma_start(out=a["qgb"], in_=qg_d[0:1, :].partition_broadcast(128))))
            wt.append(P.dma("pool", "wr", lambda e: e.dma_start(out=a["kgb"], in_=kg_d[0:1, :].partition_broadcast(128))))
            g1 = P.op("dve", lambda e: e.tensor_scalar(out=a["gt"][:, 0:8, :], in0=a["qgb"].unsqueeze(1).broadcast_to([128, 8, 128]),
                                                       scalar1=float(128.0 ** -0.5), scalar2=None, op0=ALU.mult), wt)
            g2 = P.op("dve", lambda e: e.tensor_copy(out=a["gt"][:, 8:10, :], in_=a["kgb"].unsqueeze(1).broadcast_to([128, 2, 128])), wt)
            a["wt"] = wt + [g1, g2]
            a["wo_tok"] = load_weight(a["wo"], wo_d, 4, "wo")

        def phase_attn(sq_i):
            a = astate
            wt = a["wt"]
            wqkv, wo, kT, v, gt = a["wqkv"], a["wo"], a["kT"], a["v"], a["gt"]
            r0 = sq_i * S
            nblk = S // 256
            NSUB = nblk * 2
            A.reset(a["mark"])
            xb = [A.alloc(F32, [2, 1024]) for _ in range(2)]
            hn = A.alloc(BF16, [2, 1024])
            hTs = [A.alloc(BF16, [8, 256]) for _ in range(2)]
            st = [A.alloc(F32, [8]) for _ in range(2)]
            sqb = [A.alloc(F32, [10, 128]) for _ in range(2)]
            raw = [A.alloc(F32, [10, 128]) for _ in range(3)]
            qn = [A.alloc(F32, [10, 128]) for _ in range(3)]
            ra = A.alloc(F32, [10, 128])
            rb = A.alloc(F32, [10, 128])
            qr = [A.alloc(BF16, [10, 128]) for _ in range(3)]
            ss = [A.alloc(F32, [32]) for _ in range(3)]
            qTs = [A.alloc(BF16, [8, 256]) for _ in range(2)]
            TB = (0, 1)
            QB = (2, 3, 4, 5)
            HB = (6, 7)
            NQB = len(QB)
            ld = [None] * nblk
            nrm = [None] * nblk
            hTr = [None] * nblk
            S1 = dict(hn_free=[], hT_free=[[], []], tb_free=[], xfree=[None, None], qb_free=[None] * NQB, qcount=0,
                      sqb_free=[[], []], raw_free=[[], [], []], qn_free=[[], [], []], ra_free=[], rb_free=[], qr_free=[[], [], []],
                      hbq_free=[], hbk_free=[],
                      qTs_free=[None, None])
            fr = [None] * NSUB

            def load(i):
                ld[i] = P.dma("sp", "xl%d" % (i % 2), lambda e, i=i: e.dma_start(out=xb[i % 2], in_=rows(y_d, r0 + i * 256, 2)),
                              [S1["xfree"][i % 2]])

            def norm(i):
                nrm[i] = norm_block(xb[i % 2], hn, st[i % 2], [ld[i]], S1["hn_free"])
                S1["xfree"][i % 2] = nrm[i]

            def transp(i):
                pe_tok, ev = transpose_block(hn, hTs[i % 2], gA, TB, [nrm[i]] + t_c, S1["hT_free"][i % 2], S1["tb_free"])
                S1["hn_free"] = [pe_tok]
                S1["tb_free"] = ev
                hTr[i] = ev

            FR = {}

            def front_mm(s):
                i, t = s // 2, s % 2
                hT = hTs[i % 2]
                banks = []
                mms = []
                for n in range(3):
                    k = S1["qcount"] % NQB
                    S1["qcount"] += 1
                    pv = ps[QB[k]][:]
                    mm = None
                    for c in range(8):
                        mm = P.op("pe", lambda e, c=c, n=n, t=t, pv=pv, hT=hT: e.matmul(
                            pv, lhsT=hT[:, c, t * 128:(t + 1) * 128], rhs=wqkv[:, c, n * 512:(n + 1) * 512],
                            start=(c == 0), stop=(c == 7)), hTr[i] + wt + [S1["qb_free"][k]], signal=(c == 7))
                    banks.append((k, pv))
                    mms.append(mm)
                if t == 1:
                    S1["hT_free"][i % 2] = [mms[-1]]
                sq_ = sqb[s % 2]
                sq_t = []
                vcp = None
                for n in range(3):
                    k, pv = banks[n]
                    if n < 2:
                        sq_t.append(P.op("act", lambda e, n=n, pv=pv, sq_=sq_: e.activation(
                            out=sq_[:, n * 4:(n + 1) * 4, :], in_=pv.rearrange("p (h d) -> p h d", d=128), func=AF.Square),
                            [mms[n]] + S1["sqb_free"][s % 2]))
                    else:
                        sq_t.append(P.op("act", lambda e, pv=pv, sq_=sq_: e.activation(
                            out=sq_[:, 8:10, :], in_=pv[:, 0:256].rearrange("p (h d) -> p h d", d=128), func=AF.Square),
                            [mms[n]] + S1["sqb_free"][s % 2]))
                        vcp = P.op("act", lambda e, pv=pv, s=s: e.activation(
                            out=v[:, s, :], in_=pv[:, 256:512], func=AF.Copy), [mms[n]] + a.get("v_free", []))
                raw_ = raw[s % 3]
                cp_t = []
                for n in range(3):
                    k, pv = banks[n]
                    nh = 4 if n < 2 else 2
                    tk = P.op("act", lambda e, n=n, nh=nh, pv=pv, raw_=raw_: e.activation(
                        out=raw_[:, n * 4:n * 4 + nh, :], in_=pv[:, 0:nh * 128].rearrange("p (h d) -> p h d", d=128), func=AF.Copy),
                        [mms[n]] + S1["raw_free"][s % 3])
                    cp_t.append(tk)
                    S1["qb_free"][k] = tk
                FR[s] = dict(banks=banks, mms=mms, sq_t=sq_t, vcp=vcp, cp_t=cp_t)

            def front_norm(s):
                d = FR[s]
                banks, mms, sq_t, vcp, cp_t = d["banks"], d["mms"], d["sq_t"], d["vcp"], d["cp_t"]
                raw_ = raw[s % 3]
                sst = ss[s % 3]
                sq_ = sqb[s % 2]
                rd = P.op("dve", lambda e: e.tensor_reduce(out=sst[:, 0:10], in_=sq_, axis=AX.X, op=ALU.add), sq_t)
                S1["sqb_free"][s % 2] = [rd]
                rt = P.op("act", lambda e: e.activation(out=sst[:, 10:20], in_=sst[:, 0:10], func=AF.Sqrt,
                                                        scale=1.0 / 128.0, bias=epsc[:, 0:1]), [rd])
                rc = P.op("dve", lambda e: e.reciprocal(out=sst[:, 20:30], in_=sst[:, 10:20]), [rt])
                qn_ = qn[s % 3]
                qn_t = []
                for n in range(3):
                    k, pv = banks[n]
                    nh = 4 if n < 2 else 2
                    src = raw_[:, n * 4:n * 4 + nh, :]
                    rin = sst[:, 20 + n * 4:20 + n * 4 + nh].unsqueeze(2).broadcast_to([128, nh, 128])
                    tk = P.op("dve", lambda e, n=n, nh=nh, src=src, rin=rin: e.tensor_tensor(
                        out=qn_[:, n * 4:n * 4 + nh, :], in0=src, in1=rin, op=ALU.mult),
                        [rc, cp_t[n]] + S1["qn_free"][s % 3])
                    qn_t.append(tk)
                S1["raw_free"][s % 3] = qn_t
                d["qn_t"] = qn_t

            def gain(s):
                qn_ = qn[s % 3]
                FR[s]["g"] = P.op("pool", lambda e: e.tensor_tensor(out=qn_, in0=qn_, in1=gt, op=ALU.mult), FR[s]["qn_t"])

            def back_compute(s):
                g_ = FR[s]["g"]
                qn_ = qn[s % 3]
                qv = qn_.rearrange("p h (x f i) -> p h x f i", x=2, f=2)
                rav = ra.rearrange("p h (x f i) -> p h x f i", x=2, f=2)
                rbv = rb.rearrange("p h (x f i) -> p h x f i", x=2, f=2)
                cosb = a["ropec"][:, s, :, :].unsqueeze(1).broadcast_to([128, 10, 2, 32])
                sinb = a["ropes"][:, s, :, :].unsqueeze(1).broadcast_to([128, 10, 2, 32])
                nsinb = a["ropen"][:, s, :, :].unsqueeze(1).broadcast_to([128, 10, 2, 32])
                o1a = P.op("dve", lambda e: e.tensor_tensor(out=rav[:, :, :, 0, :], in0=qv[:, :, :, 0, :], in1=cosb, op=ALU.mult),
                           [g_] + S1["ra_free"])
                o1 = P.op("dve", lambda e: e.tensor_tensor(out=rav[:, :, :, 1, :], in0=qv[:, :, :, 1, :], in1=cosb, op=ALU.mult),
                          [g_] + S1["ra_free"])
                o2 = P.op("pool", lambda e: e.tensor_tensor(out=rbv[:, :, :, 0, :], in0=qv[:, :, :, 1, :], in1=nsinb, op=ALU.mult),
                          [g_] + S1["rb_free"])
                o3 = P.op("pool", lambda e: e.tensor_tensor(out=rbv[:, :, :, 1, :], in0=qv[:, :, :, 0, :], in1=sinb, op=ALU.mult),
                          [g_] + S1["rb_free"])
                S1["qn_free"][s % 3] = [o1a, o1, o2, o3]
                qrr = qr[s % 3]
                o4 = P.op("dve", lambda e: e.tensor_tensor(out=qrr, in0=ra, in1=rb, op=ALU.add), [o1a, o1, o2, o3] + S1["qr_free"][s % 3])
                S1["ra_free"] = [o4]
                S1["rb_free"] = [o4]
                FR[s]["o4"] = o4

            def back_pe(s):
                i, t = s // 2, s % 2
                o4 = FR[s]["o4"]
                qrr = qr[s % 3]
                pvq = psb(HB[0])
                pvk = psb(HB[1])
                tkh = None
                for hh in range(8):
                    tkh = P.op("pe", lambda e, hh=hh: e.transpose(out=pvq[:, hh * 128:(hh + 1) * 128], in_=qrr[:, hh, :], identity=ident),
                               [o4] + S1["hbq_free"], signal=(hh == 7))
                tkk = None
                for hh in range(2):
                    tkk = P.op("pe", lambda e, hh=hh: e.transpose(out=pvk[:, hh * 128:(hh + 1) * 128], in_=qrr[:, 8 + hh, :], identity=ident),
                               [o4] + S1["hbk_free"], signal=(hh == 1))
                S1["qr_free"][s % 3] = [tkk]
                qts = qTs[i % 2]
                e1 = P.op("act", lambda e: e.activation(
                    out=qts[:, :, t * 128:(t + 1) * 128], in_=pvq.rearrange("p (h k) -> p h k", k=128), func=AF.Copy),
                    [tkh, S1["qTs_free"][i % 2]])
                S1["hbq_free"] = [e1]
                e2 = P.op("act", lambda e: e.activation(
                    out=kT[:, :, s * 128:(s + 1) * 128], in_=pvk[:, 0:256].rearrange("p (h k) -> p h k", k=128), func=AF.Copy),
                    [tkk] + a.get("k_free", []))
                S1["hbk_free"] = [e2]
                del FR[s]
                return e1

            load(0)
            load(1)
            norm(0)
            transp(0)
            norm(1)
            transp(1)
            if nblk > 2:
                load(2)
                norm(2)
            front_mm(0)
            front_norm(0)
            front_mm(1)
            front_norm(1)
            gain(0)
            qst = [None] * nblk
            e_prev = {}
            for s in range(NSUB + 1):
                if s < NSUB:
                    back_compute(s)
                    if s + 1 < NSUB:
                        gain(s + 1)
                    if s + 2 < NSUB:
                        front_mm(s + 2)
                if s >= 1:
                    sp = s - 1
                    ip, tp = sp // 2, sp % 2
                    e1 = back_pe(sp)
                    if tp == 0:
                        e_prev[ip] = e1
                    else:
                        qb512, hf = ip // 2, ip % 2
                        qst[ip] = P.dma("sp", "qs%d" % (ip % 2), lambda e, ip=ip, qb512=qb512, hf=hf: e.dma_start(
                            out=qscr[qb512].rearrange("p (h k) -> p h k", k=512)[:, :, hf * 256:(hf + 1) * 256], in_=qTs[ip % 2]),
                            [e_prev[ip], e1])
                        S1["qTs_free"][ip % 2] = qst[ip]
                if s < NSUB:
                    if s + 2 < NSUB:
                        front_norm(s + 2)
                    i, t = s // 2, s % 2
                    if t == 1 and i + 2 < nblk:
                        transp(i + 2)
                        if i + 3 < nblk:
                            load(i + 3)
                            norm(i + 3)
            P.barrier()
            if "DBGQ" in phases:
                return

            A.reset(a["mark"])
            qT = [A.alloc(BF16, [8, 512]) for _ in range(2)]
            NPT = 6
            pT = [A.alloc(BF16, [512]) for _ in range(NPT)]
            oT = [A.alloc(BF16, [8, 512]) for _ in range(2)]
            rec = [A.alloc(F32, [512]) for _ in range(2)]
            accd = [A.alloc(F32, [512]) for _ in range(2)]
            tra = [A.alloc(BF16, [512]) for _ in range(2)]
            trb = [A.alloc(BF16, [512]) for _ in range(2)]
            trc = [A.alloc(BF16, [512]) for _ in range(2)]
            trd = [A.alloc(BF16, [512]) for _ in range(2)]
            onesf = A.alloc(F32, [128])
            xr = [A.alloc(F32, [4, 1024]) for _ in range(2)]
            SB = (0, 1, 2, 3)
            OB = ((4, 5), (6, 7))
            NQ = S // 512
            S2 = dict(sb_free=[None] * 4, pT_free=[[] for _ in range(NPT)], scount=0, pcount=0, ob_free=[[None, None], [None, None]],
                      oT_free=[[], []], rec_free=[None, None], qT_free=[None, None], xr_free=[None, None], it=0,
                      acc_free=[[], []], tr_free=[], oT_ready=[[], []])
            pending_fin = []
            pending_wo = []
            pending_tail = []
            pending_tail2 = []
            ldq = [None] * NQ
            ldx = [None] * NQ
            stx = [None] * NQ
            t_of = P.op("pool", lambda e: e.memset(onesf, 1.0))

            def loadqT(qb):
                ldq[qb] = P.dma("sp", "ql%d" % (qb % 2), lambda e, qb=qb: e.dma_start(
                    out=qT[qb % 2], in_=qscr[qb].rearrange("p (h k) -> p h k", k=512)), [S2["qT_free"][qb % 2]])

            def loadxr(qb):
                ldx[qb] = P.dma("sp", "xr%d" % (qb % 2), lambda e, qb=qb: e.dma_start(
                    out=xr[qb % 2], in_=rows(y_d, r0 + qb * 512, 4)), [S2["xr_free"][qb % 2]])

            def loadq(qb):
                loadqT(qb)
                loadxr(qb)

            loadq(0)
            if NQ > 1:
                loadq(1)
            EXS = {}

            def issue_s(qb_, h_, kc):
                g_ = h_ // 4
                qTb_ = qT[qb_ % 2]
                k = S2["scount"] % 4
                S2["scount"] += 1
                pk = S2["pcount"] % NPT
                S2["pcount"] += 1
                pv = ps[SB[k]][:]
                mm = P.op("pe", lambda e: e.matmul(
                    pv, lhsT=kT[:, g_, kc * 128:(kc + 1) * 128], rhs=qTb_[:, h_, :], start=True, stop=True),
                    [ldq[qb_], S2["sb_free"][k]])
                ptile = pT[pk]
                ex = P.op("act", lambda e: e.activation(out=ptile, in_=pv, func=AF.Exp),
                          [mm] + S2["pT_free"][pk])
                S2["sb_free"][k] = ex
                EXS[(qb_, h_, kc)] = (pk, ex, ptile)

            def tile_after(qb_, h_, kc, d):
                idx = (qb_ * 8 + h_) * 32 + kc + d
                if idx >= NQ * 8 * 32:
                    return None
                return (idx // 256, (idx // 32) % 8, idx % 32)

            for kc0 in range(3):
                issue_s(0, 0, kc0)
            for qb in range(NQ):
                qTb = qT[qb % 2]
                oTb = oT[qb % 2]
                for h in range(8):
                    g = h // 4
                    it = S2["it"]
                    S2["it"] += 1
                    ob_o, ob_s = OB[it % 2]
                    pvo = ps[ob_o][:]
                    pvs = ps[ob_s][:]
                    ad = accd[it % 2]
                    exs = {}
                    T = dict(a=[None, None], b=[None, None], c=[None, None], d={}, acc=None, accs=[])

                    def emit_c(m, T=T, it=it):
                        ta_, tb__, tc_ = tra[m % 2], trb[m % 2], trc[m % 2]
                        T["c"][m % 2] = P.op("dve", lambda e, ta_=ta_, tb__=tb__, tc_=tc_: e.tensor_tensor(out=tc_, in0=ta_, in1=tb__, op=ALU.add),
                                             [T["a"][m % 2], T["b"][m % 2], T["d"][(m - 2) // 2] if m >= 2 else None] + S2["tr_free"])

                    def emit_d(j, T=T, it=it):
                        td_ = trd[j % 2]
                        T["d"][j] = P.op("dve", lambda e, td_=td_: e.tensor_tensor(out=td_, in0=trc[0], in1=trc[1], op=ALU.add),
                                         [T["c"][0], T["c"][1], T["accs"][j - 2] if j >= 2 else None] + S2["tr_free"])

                    def emit_acc(j, T=T, it=it, pvs=pvs):
                        td_ = trd[j % 2]
                        T["acc"] = P.op("pe", lambda e, td_=td_, pvs=pvs, j=j: e.matmul(pvs, lhsT=ones, rhs=td_, start=(j == 0), stop=(j == 3)),
                                        [T["d"][j], S2["ob_free"][it % 2][1]])
                        T["accs"].append(T["acc"])

                    def issue_pv(kc):
                        pk, ex, ptile = EXS[(qb, h, kc)]
                        mm = P.op("pe", lambda e, kc=kc, ptile=ptile, pvo=pvo, g=g: e.matmul(
                            pvo, lhsT=v[:, kc, g * 128:(g + 1) * 128], rhs=ptile, start=(kc == 0), stop=(kc == 31)),
                            [ex, S2["ob_free"][it % 2][0]], signal=True)
                        m, r4 = kc // 4, kc % 4
                        if r4 == 0 or r4 == 2:
                            S2["pT_free"][pk] = [mm]
                            EXS[(qb, h, kc)] = (pk, ex, ptile, mm)
                        else:
                            pk0, ex0, ptile0, mm0 = EXS[(qb, h, kc - 1)]
                            dst = (tra if r4 == 1 else trb)[m % 2]
                            sm = P.op("dve", lambda e, ptile=ptile, ptile0=ptile0, dst=dst: e.tensor_tensor(out=dst, in0=ptile0, in1=ptile, op=ALU.add),
                                      [ex, ex0, T["c"][m % 2] if m >= 2 else None] + S2["tr_free"])
                            (T["a"] if r4 == 1 else T["b"])[m % 2] = sm
                            S2["pT_free"][pk] = [mm, sm]
                            S2["pT_free"][pk0] = [mm0, sm]
                            if r4 == 1 and m >= 1:
                                emit_c(m - 1)
                            if r4 == 3 and m >= 1:
                                if (m - 1) % 2 == 1:
                                    emit_d((m - 1) // 2)
                                elif m >= 3:
                                    emit_acc((m - 3) // 2)
                        return mm

                    mm_last = None
                    for kc in range(32):
                        nt = tile_after(qb, h, kc, 3)
                        if nt is not None:
                            issue_s(*nt)
                        mm_last = issue_pv(kc)
                        if kc == 2 and pending_tail:
                            pending_tail.pop(0)()
                        if kc == 4 and pending_tail2:
                            pending_tail2.pop(0)()
                        if kc in (6, 8, 10, 12, 13) and pending_fin:
                            pending_fin.pop(0)()
                        if kc == 16 and pending_wo and not pending_fin:
                            pending_wo.pop(0)()
                    S2["tr_free"] = [T["c"][0], T["d"][2], T["acc"]]

                    def tail(T=T, emit_c=emit_c, emit_d=emit_d):
                        emit_c(7)
                        emit_d(3)
                        S2["tr_free"] = [T["c"][1], T["d"][3], T["acc"]]
                    pending_tail.append(tail)

                    def tail2(T=T, emit_acc=emit_acc):
                        emit_acc(3)
                        S2["tr_free"] = [T["c"][1], T["d"][3], T["acc"]]
                    pending_tail2.append(tail2)

                    def make_fin(it=it, h=h, qb=qb, oTb=oTb, pvo=pvo, pvs=pvs, T=T, mm_last=mm_last):
                        rc_ = rec[it % 2]
                        st_ = dict(r=[])

                        def piece(j):
                            def run():
                                st_["r"].append(P.op("dve", lambda e: e.reciprocal(out=rc_[:, j * 128:(j + 1) * 128], in_=pvs[:, j * 128:(j + 1) * 128]),
                                                     [T["acc"], S2["rec_free"][it % 2]]))
                            return run

                        def mult():
                            d1 = st_["r"][-1]
                            d2 = P.op("dve", lambda e: e.tensor_tensor(out=oTb[:, h, :], in0=pvo, in1=rc_, op=ALU.mult),
                                      st_["r"] + [mm_last] + S2["oT_free"][qb % 2])
                            S2["rec_free"][it % 2] = d2
                            S2["ob_free"][it % 2] = [d2, d1]
                            S2["acc_free"][it % 2] = [T["acc"]]
                            S2["oT_ready"][qb % 2].append(d2)
                        return [piece(0), piece(1), piece(2), piece(3), mult]
                    pending_fin.extend(make_fin())
                S2["qT_free"][qb % 2] = P.last("pe")

                def make_wo_groups(qb=qb, oTb=oTb):
                    st_ = dict(evs=[], mm=None, rdy=None)

                    def group(gi):
                        def run():
                            if st_["rdy"] is None:
                                st_["rdy"] = S2["oT_ready"][qb % 2]
                                S2["oT_ready"][qb % 2] = []
                                if qb + 2 < NQ:
                                    loadqT(qb + 2)
                            t, half = gi // 2, gi % 2
                            k = S2["scount"] % 4
                            S2["scount"] += 1
                            pv = ps[SB[k]][:]
                            mm = None
                            for h in range(8):
                                mm = P.op("pe", lambda e, h=h: e.matmul(
                                    pv, lhsT=oTb[:, h, t * 128:(t + 1) * 128], rhs=wo[:, h, half * 512:(half + 1) * 512],
                                    start=(h == 0), stop=(h == 7)), st_["rdy"] + [S2["sb_free"][k]] + a["wo_tok"], signal=(h == 7))
                            xs = xr[qb % 2][:, t, half * 512:(half + 1) * 512]
                            ev = P.op("dve", lambda e: e.tensor_tensor(out=xs, in0=pv, in1=xs, op=ALU.add), [mm, ldx[qb]])
                            S2["sb_free"][k] = ev
                            st_["evs"].append(ev)
                            if gi == 7:
                                S2["oT_free"][qb % 2] = [mm]
                                stx[qb] = P.dma("sp", "xw%d" % (qb % 2), lambda e: e.dma_start(out=rows(y_d, r0 + qb * 512, 4), in_=xr[qb % 2]), st_["evs"])
                                S2["xr_free"][qb % 2] = stx[qb]
                                if qb + 2 < NQ:
                                    loadxr(qb + 2)
                        return run
                    return [group(gi) for gi in range(8)]
                pending_wo.extend(make_wo_groups())
            while pending_tail:
                pending_tail.pop(0)()
            while pending_tail2:
                pending_tail2.pop(0)()
            while pending_fin:
                pending_fin.pop(0)()
            while pending_wo:
                pending_wo.pop(0)()
            a["v_free"] = [P.last("pe")]
            a["k_free"] = [P.last("pe")]
            P.barrier()

        if "C" in phases:
            cts = []
            for i in range(T // 256):
                cts.append(P.dma("sp", "cpy", lambda e, i=i: e.dma_start(out=y_d[i * 256:(i + 1) * 256, :], in_=x_d[i * 256:(i + 1) * 256, :])))
            P.barrier()
        if "F" in phases:
            phase_fourier_setup()
            for s_i in range(n_seq):
                phase_fourier(s_i, y_d if x_from_y else x_d)
        if "M0" in phases:
            phase_mlp(0, final=False)
        if "A" in phases:
            try:
                phase_attn_setup()
                chk(0)
                for s_i in range(n_seq):
                    phase_attn(s_i)
            except _Stop:
                pass
        if "M1" in phases:
            phase_mlp(1, final=True)
        P.barrier()

        block = es.enter_context(nc.Block())
        P.replay(block)
        _NC_CACHE["lastP"] = P
    return nc


_NC_CACHE = {}


def _common_inputs(fourier_norm, fourier_w_out, attn_norm, attn_w_qkv, attn_q_norm, attn_k_norm, attn_w_o,
                   mlp_norm, mlp_w_up, mlp_w_down, final_norm):
    f = lambda a: np.ascontiguousarray(np.asarray(a, dtype=np.float32))
    c = host_consts()
    m = dict(
        g_f=f(np.asarray(fourier_norm)[0].reshape(8, 128).T),
        g_a=f(np.asarray(attn_norm)[0].reshape(8, 128).T),
        g_m=f(np.concatenate([np.asarray(mlp_norm)[0].reshape(8, 128).T, np.asarray(mlp_norm)[1].reshape(8, 128).T], axis=1)),
        g_fin=f(np.asarray(final_norm).reshape(1, D)),
        qg=f(np.asarray(attn_q_norm).reshape(1, 128)),
        kg=f(np.asarray(attn_k_norm).reshape(1, 128)),
        w_fout=f(np.asarray(fourier_w_out)[0]),
        w_qkv=f(np.asarray(attn_w_qkv)[0]),
        w_o=f(np.asarray(attn_w_o)[0]),
        w_up=f(mlp_w_up),
        w_dn=f(mlp_w_down),
    )
    m.update(c)
    return m


def kernel(x_prompt, x_sample, fourier_norm, fourier_w_out, attn_norm, attn_w_qkv, attn_q_norm, attn_k_norm,
           attn_w_o, mlp_norm, mlp_w_up, mlp_w_down, final_norm):
    x_prompt = np.asarray(x_prompt, dtype=np.float32)
    x_sample = np.asarray(x_sample, dtype=np.float32)
    nb_p, nb_s = x_prompt.shape[0], x_sample.shape[0]
    x_all = np.concatenate([x_prompt.reshape(nb_p * S, D), x_sample.reshape(nb_s * S, D)], axis=0)
    common = _common_inputs(fourier_norm, fourier_w_out, attn_norm, attn_w_qkv, attn_q_norm, attn_k_norm, attn_w_o,
                            mlp_norm, mlp_w_up, mlp_w_down, final_norm)
    if "nc" not in _NC_CACHE:
        _NC_CACHE["nc"] = build_program()
    nc = _NC_CACHE["nc"]
    rows_per_core = SEQ_PER_CORE * S
    in_maps = []
    for c in range(NCORES):
        m = dict(common)
        m["x"] = x_all[c * rows_per_core:(c + 1) * rows_per_core]
        in_maps.append(m)
    res = run_bass_kernel_spmd(nc, in_maps, core_ids=list(range(NCORES)))
    y_all = np.concatenate([np.asarray(r["y"], dtype=np.float32) for r in res.results], axis=0)
    y_prompt = y_all[:nb_p * S].reshape(nb_p, S, D)
    y_sample = y_all[nb_p * S:].reshape(nb_s, S, D)
    return (y_prompt, y_sample)
```
